# Optimizing a Trainium2 kernel written in Bass

```python
import math
import jax, jax.numpy as jnp
from jax import lax
import numpy as np

D_MODEL = 2048
BATCH = 1
SEQ = 8192
DEPTH = 4
DEC_BATCH = 4
DEC_SEQ = 2048
PAST_LEN = 128

D_MIX = D_MODEL
D_FF = 4 * D_MODEL
NORM_EPS = 1e-6
CONV_WIDTH = 4
CONV_PAD = (1, 2)

LRU_HEADS = 8
LRU_HEAD_DIM = 64
D_LRU = LRU_HEADS * LRU_HEAD_DIM
LRU_C = 8.0

RWKV_HEADS = 12
RWKV_HEAD_DIM = 64
D_RWKV = RWKV_HEADS * RWKV_HEAD_DIM
RWKV_DECAY_RANK = 96
RWKV_A_RANK = 96
RWKV_GATE_RANK = 256
RWKV_GN_EPS = 64e-5

SSD_HEADS = 12
SSD_HEAD_DIM = 64
D_SSD = SSD_HEADS * SSD_HEAD_DIM
SSD_GROUPS = 4
SSD_HEADS_PER_GROUP = SSD_HEADS // SSD_GROUPS
SSD_STATE = 128
SSD_CHUNK = 128
SSD_BC = SSD_GROUPS * SSD_STATE
SSD_CONV_DIM = D_SSD + 2 * SSD_BC

LRU_COLS = 2 * D_LRU
RWKV_COLS = 3 * D_RWKV + 2 * (RWKV_DECAY_RANK + RWKV_A_RANK) + RWKV_GATE_RANK
SSD_COLS = D_SSD + SSD_CONV_DIM + 2 * SSD_HEADS
IN_COLS = LRU_COLS + RWKV_COLS + SSD_COLS

kernel_name = 'hybrid_bidir_lru_rwkv7_ssd_encoder'


def rms_norm(x, g):
    xf = x.astype(jnp.float32)
    y = xf * lax.rsqrt(jnp.mean(xf * xf, axis=-1, keepdims=True) + NORM_EPS)
    return (y * g.astype(jnp.float32)).astype(x.dtype)


def centred_dwconv(x, w, b):
    c = x.shape[-1]
    y = lax.conv_general_dilated(x, w[:, None, :].astype(x.dtype), window_strides=(1,),
                                 padding=(CONV_PAD,), dimension_numbers=('NWC', 'WIO', 'NWC'),
                                 feature_group_count=c)
    return y + b


def centred_shift_mix(p, mu):
    prev = jnp.pad(p[:, :-1], ((0, 0), (1, 0), (0, 0)))
    nxt = jnp.pad(p[:, 1:], ((0, 0), (0, 1), (0, 0)))
    return p + mu[0] * (prev - p) + mu[1] * (nxt - p)


def linear_scan(a, bx, reverse):
    def combine(left, right):
        a_l, b_l = left
        a_r, b_r = right
        return a_l * a_r, a_r * b_l + b_r
    _, h = lax.associative_scan(combine, (a, bx), axis=1, reverse=reverse)
    return h


def rglru_mixer(p, conv_w, conv_b, gate_w, gate_b, lam):
    bsz, seq, _ = p.shape
    gate_branch, xb = p[..., :D_LRU], p[..., D_LRU:]
    xc = centred_dwconv(xb, conv_w, conv_b)
    xh = xc.reshape(bsz, seq, LRU_HEADS, LRU_HEAD_DIM)
    gates = jax.nn.sigmoid(jnp.einsum('blhi,dghij->dgblhj', xh, gate_w)
                           + gate_b[:, :, None, None]).astype(jnp.float32)
    gates = gates.reshape(2, 2, bsz, seq, D_LRU)
    r, i = gates[:, 0], gates[:, 1]
    log_a = -LRU_C * r * jax.nn.softplus(-lam.astype(jnp.float32))[:, None, None, :]
    a = jnp.exp(log_a)
    bx = jnp.sqrt(-jnp.expm1(2.0 * log_a)) * (i * xc.astype(jnp.float32)[None])
    h = linear_scan(a[0], bx[0], False) + linear_scan(a[1], bx[1], True)
    return h.astype(p.dtype) * jax.nn.gelu(gate_branch)


def rwkv7_scan(r, w, k, v, z, b, reverse):
    bsz, seq, nh, n = r.shape

    def step(s, inp):
        r_t, w_t, k_t, v_t, z_t, b_t = inp
        sa = jnp.einsum('bhvk,bhk->bhv', s, z_t)
        s = s * w_t[:, :, None, :] + sa[..., None] * b_t[:, :, None, :] + v_t[..., None] * k_t[:, :, None, :]
        return s, jnp.einsum('bhvk,bhk->bhv', s, r_t)

    s0 = jnp.zeros((bsz, nh, n, n), jnp.float32)
    xs = tuple(jnp.moveaxis(t, 1, 0) for t in (r, w, k, v, z, b))
    _, y = lax.scan(step, s0, xs, reverse=reverse)
    return jnp.moveaxis(y, 0, 1)


def rwkv7_mixer(p, mu, w0, w_up, a0, a_up, g_up, k_k, k_a, r_k, ln_g, ln_b):
    bsz, seq, _ = p.shape
    f32 = jnp.float32
    heads = lambda t: t.reshape(bsz, seq, RWKV_HEADS, RWKV_HEAD_DIM)
    p = centred_shift_mix(p, mu)
    o = 0
    r = p[..., o:o + D_RWKV]; o += D_RWKV
    k = p[..., o:o + D_RWKV]; o += D_RWKV
    v = p[..., o:o + D_RWKV]; o += D_RWKV
    w_lo = p[..., o:o + 2 * RWKV_DECAY_RANK].reshape(bsz, seq, 2, RWKV_DECAY_RANK); o += 2 * RWKV_DECAY_RANK
    a_lo = p[..., o:o + 2 * RWKV_A_RANK].reshape(bsz, seq, 2, RWKV_A_RANK); o += 2 * RWKV_A_RANK
    g_lo = p[..., o:o + RWKV_GATE_RANK]
    w_pre = (w0[:, None, None] + jnp.einsum('bldr,drc->dblc', jnp.tanh(w_lo), w_up)).astype(f32)
    decay = jnp.exp(-jnp.exp(-jax.nn.softplus(-w_pre) - 0.5))
    a = jax.nn.sigmoid((a0[:, None, None] + jnp.einsum('bldr,drc->dblc', a_lo, a_up)).astype(f32))
    g = jax.nn.sigmoid(g_lo) @ g_up
    kk = heads((k * k_k).astype(f32))
    kk = kk / jnp.maximum(jnp.sqrt(jnp.sum(kk * kk, axis=-1, keepdims=True)), 1e-12)
    k_dir = k.astype(f32)[None] * (1.0 + (a - 1.0) * k_a.astype(f32))
    rh = heads(r.astype(f32))
    vh = heads(v.astype(f32))
    y = (rwkv7_scan(rh, heads(decay[0]), heads(k_dir[0]), vh, -kk, kk * heads(a[0]), False)
         + rwkv7_scan(rh, heads(decay[1]), heads(k_dir[1]), vh, -kk, kk * heads(a[1]), True))
    mean = jnp.mean(y, axis=-1, keepdims=True)
    var = jnp.mean(jnp.square(y - mean), axis=-1, keepdims=True)
    y = ((y - mean) * lax.rsqrt(var + RWKV_GN_EPS)).reshape(bsz, seq, D_RWKV)
    y = y * ln_g.astype(f32) + ln_b.astype(f32)
    bonus = jnp.sum(rh * heads(k_dir[0] + k_dir[1]) * r_k.astype(f32), axis=-1, keepdims=True) * vh
    return (y + bonus.reshape(bsz, seq, D_RWKV)).astype(p.dtype) * g


def ssd_chunked(x, dt, a_head, bm, cm):
    bsz, seq, nh, hp = x.shape
    g, r, q = SSD_GROUPS, SSD_HEADS_PER_GROUP, SSD_CHUNK
    nc = seq // q
    xd = (x * dt[..., None]).reshape(bsz, nc, q, g, r, hp)
    a_cum = jnp.cumsum((dt * a_head).reshape(bsz, nc, q, g, r), axis=2)
    bc = bm.reshape(bsz, nc, q, g, SSD_STATE)
    cc = cm.reshape(bsz, nc, q, g, SSD_STATE)
    a_t = jnp.moveaxis(a_cum, 2, -1)
    seg = a_t[..., :, None] - a_t[..., None, :]
    lower = jnp.tril(jnp.ones((q, q), dtype=bool))
    decay_ij = jnp.exp(jnp.where(lower, seg, -jnp.inf))
    cb = jnp.einsum('bcign,bcjgn->bcgij', cc, bc)
    y_diag = jnp.einsum('bcgij,bcgrij,bcjgrp->bcigrp', cb, decay_ij, xd)
    decay_to_end = jnp.exp(a_cum[:, :, -1:] - a_cum)
    chunk_states = jnp.einsum('bcjgn,bcjgr,bcjgrp->bcgrpn', bc, decay_to_end, xd)
    chunk_decay = jnp.exp(a_cum[:, :, -1])

    def carry(h, inp):
        s_c, d_c = inp
        return h * d_c[..., None, None] + s_c, h

    h0 = jnp.zeros((bsz, g, r, hp, SSD_STATE), jnp.float32)
    _, h_in = lax.scan(carry, h0, (jnp.moveaxis(chunk_states, 1, 0), jnp.moveaxis(chunk_decay, 1, 0)))
    h_in = jnp.moveaxis(h_in, 0, 1)
    y_off = jnp.einsum('bcign,bcgrpn,bcigr->bcigrp', cc, h_in, jnp.exp(a_cum))
    return (y_diag + y_off).reshape(bsz, seq, nh, hp)


def ssd_mixer(p, conv_w, conv_b, dt_bias, a_log, d_skip, norm_g):
    bsz, seq, _ = p.shape
    f32 = jnp.float32
    z = p[..., :D_SSD]
    xbc = p[..., D_SSD:D_SSD + SSD_CONV_DIM]
    dt_raw = p[..., D_SSD + SSD_CONV_DIM:].reshape(bsz, seq, 2, SSD_HEADS)
    xbc = jax.nn.silu(centred_dwconv(xbc, conv_w, conv_b)).astype(f32)
    xs = xbc[..., :D_SSD].reshape(bsz, seq, SSD_HEADS, SSD_HEAD_DIM)
    bm = xbc[..., D_SSD:D_SSD + SSD_BC].reshape(bsz, seq, SSD_GROUPS, SSD_STATE)
    cm = xbc[..., D_SSD + SSD_BC:].reshape(bsz, seq, SSD_GROUPS, SSD_STATE)
    dt = jax.nn.softplus(dt_raw.astype(f32) + dt_bias.astype(f32))
    a_head = -jnp.exp(a_log.astype(f32))
    flip = lambda t: jnp.flip(t, axis=1)
    y_f = ssd_chunked(xs, dt[:, :, 0], a_head[0], bm, cm)
    y_b = flip(ssd_chunked(flip(xs), flip(dt[:, :, 1]), a_head[1], flip(bm), flip(cm)))
    y = y_f + y_b + d_skip.astype(f32)[:, None] * xs
    y = y.reshape(bsz, seq, D_SSD).astype(p.dtype)
    return rms_norm(y * jax.nn.silu(z), norm_g)


def encoder_trunk(x, w):
    (norm1_g, w_in, lru_conv_w, lru_conv_b, lru_gate_w, lru_gate_b, lru_lambda,
     rwkv_mu, rwkv_w0, rwkv_w_up, rwkv_a0, rwkv_a_up, rwkv_g_up, rwkv_k_k, rwkv_k_a,
     rwkv_r_k, rwkv_ln_g, rwkv_ln_b, ssd_conv_w, ssd_conv_b, ssd_dt_bias, ssd_a_log,
     ssd_d, ssd_norm_g, w_out, norm2_g, mlp_w1, mlp_w2, final_norm_g) = w
    for l in range(DEPTH):
        u = rms_norm(x, norm1_g[l])
        p = u @ w_in[l]
        y_lru = rglru_mixer(p[..., :LRU_COLS], lru_conv_w[l], lru_conv_b[l],
                            lru_gate_w[l], lru_gate_b[l], lru_lambda[l])
        y_rwkv = rwkv7_mixer(p[..., LRU_COLS:LRU_COLS + RWKV_COLS], rwkv_mu[l], rwkv_w0[l],
                             rwkv_w_up[l], rwkv_a0[l], rwkv_a_up[l], rwkv_g_up[l], rwkv_k_k[l],
                             rwkv_k_a[l], rwkv_r_k[l], rwkv_ln_g[l], rwkv_ln_b[l])
        y_ssd = ssd_mixer(p[..., LRU_COLS + RWKV_COLS:], ssd_conv_w[l], ssd_conv_b[l],
                          ssd_dt_bias[l], ssd_a_log[l], ssd_d[l], ssd_norm_g[l])
        x = x + jnp.concatenate([y_lru, y_rwkv, y_ssd], axis=-1) @ w_out[l]
        u = rms_norm(x, norm2_g[l])
        x = x + jnp.square(jax.nn.relu(u @ mlp_w1[l])) @ mlp_w2[l]
    return rms_norm(x, final_norm_g)


def setup_inputs(seed: int = 0) -> dict:
    key = jax.random.key(seed)
    ks = iter(jax.random.split(key, 48))
    f32 = jnp.float32
    L_ = DEPTH

    def nrm(shape, scale):
        return scale * jax.random.normal(next(ks), shape, f32)

    def unif(shape, lo, hi):
        return jax.random.uniform(next(ks), shape, f32, lo, hi)

    x_prompt = nrm((BATCH, SEQ, D_MODEL), 1.0)
    x_sample = nrm((DEC_BATCH, DEC_SEQ, D_MODEL), 1.0)
    lru_s = unif((L_, 2, D_LRU), 0.9, 0.999) ** (1.0 / LRU_C)
    lru_lambda = jnp.log(lru_s) - jnp.log1p(-lru_s)
    dt0 = jnp.exp(unif((L_, 2, SSD_HEADS), math.log(1e-3), math.log(1e-1)))
    ssd_dt_bias = dt0 + jnp.log(-jnp.expm1(-dt0))
    return {
        'x_prompt': x_prompt,
        'x_sample': x_sample,
        'norm1_g': 1.0 + nrm((L_, D_MODEL), 0.02),
        'w_in': nrm((L_, D_MODEL, IN_COLS), D_MODEL ** -0.5),
        'lru_conv_w': nrm((L_, CONV_WIDTH, D_LRU), CONV_WIDTH ** -0.5),
        'lru_conv_b': nrm((L_, D_LRU), 0.02),
        'lru_gate_w': nrm((L_, 2, 2, LRU_HEADS, LRU_HEAD_DIM, LRU_HEAD_DIM), LRU_HEAD_DIM ** -0.5),
        'lru_gate_b': nrm((L_, 2, 2, LRU_HEADS, LRU_HEAD_DIM), 0.02),
        'lru_lambda': lru_lambda,
        'rwkv_mu': unif((L_, 2, RWKV_COLS), 0.0, 0.4),
        'rwkv_w0': unif((L_, 2, D_RWKV), -6.0, 1.0),
        'rwkv_w_up': nrm((L_, 2, RWKV_DECAY_RANK, D_RWKV), 0.1 * RWKV_DECAY_RANK ** -0.5),
        'rwkv_a0': nrm((L_, 2, D_RWKV), 0.1),
        'rwkv_a_up': nrm((L_, 2, RWKV_A_RANK, D_RWKV), 0.1 * RWKV_A_RANK ** -0.5),
        'rwkv_g_up': nrm((L_, RWKV_GATE_RANK, D_RWKV), RWKV_GATE_RANK ** -0.5),
        'rwkv_k_k': 0.85 + nrm((L_, D_RWKV), 0.02),
        'rwkv_k_a': 1.0 + nrm((L_, D_RWKV), 0.02),
        'rwkv_r_k': nrm((L_, RWKV_HEADS, RWKV_HEAD_DIM), 0.1),
        'rwkv_ln_g': 1.0 + nrm((L_, D_RWKV), 0.02),
        'rwkv_ln_b': nrm((L_, D_RWKV), 0.02),
        'ssd_conv_w': nrm((L_, CONV_WIDTH, SSD_CONV_DIM), CONV_WIDTH ** -0.5),
        'ssd_conv_b': nrm((L_, SSD_CONV_DIM), 0.02),
        'ssd_dt_bias': ssd_dt_bias,
        'ssd_a_log': jnp.log(unif((L_, 2, SSD_HEADS), 1.0, 16.0)),
        'ssd_d': 1.0 + nrm((L_, SSD_HEADS), 0.02),
        'ssd_norm_g': 1.0 + nrm((L_, D_SSD), 0.02),
        'w_out': nrm((L_, D_MIX, D_MODEL), D_MIX ** -0.5),
        'norm2_g': 1.0 + nrm((L_, D_MODEL), 0.02),
        'mlp_w1': nrm((L_, D_MODEL, D_FF), D_MODEL ** -0.5),
        'mlp_w2': nrm((L_, D_FF, D_MODEL), D_FF ** -0.5),
        'final_norm_g': 1.0 + nrm((D_MODEL,), 0.02),
    }


def reference(x_prompt, x_sample, norm1_g, w_in, lru_conv_w, lru_conv_b, lru_gate_w, lru_gate_b,
              lru_lambda, rwkv_mu, rwkv_w0, rwkv_w_up, rwkv_a0, rwkv_a_up, rwkv_g_up, rwkv_k_k,
              rwkv_k_a, rwkv_r_k, rwkv_ln_g, rwkv_ln_b, ssd_conv_w, ssd_conv_b, ssd_dt_bias,
              ssd_a_log, ssd_d, ssd_norm_g, w_out, norm2_g, mlp_w1, mlp_w2, final_norm_g):
    weights = (norm1_g, w_in, lru_conv_w, lru_conv_b, lru_gate_w, lru_gate_b, lru_lambda,
               rwkv_mu, rwkv_w0, rwkv_w_up, rwkv_a0, rwkv_a_up, rwkv_g_up, rwkv_k_k, rwkv_k_a,
               rwkv_r_k, rwkv_ln_g, rwkv_ln_b, ssd_conv_w, ssd_conv_b, ssd_dt_bias, ssd_a_log,
               ssd_d, ssd_norm_g, w_out, norm2_g, mlp_w1, mlp_w2, final_norm_g)
    y_prompt = encoder_trunk(x_prompt, weights)
    y_sample = encoder_trunk(x_sample, weights)
    return (y_prompt, y_sample)
```

```python
import math
from contextlib import ExitStack

import numpy as np
import concourse.bass as bass
import concourse.mybir as mybir
from concourse.bass_utils import run_bass_kernel_spmd

F32 = mybir.dt.float32
BF16 = mybir.dt.bfloat16
AF = mybir.ActivationFunctionType
ALU = mybir.AluOpType

NCORE = 8
D = 2048
DFF = 8192
KC = D // 128
D_LRU = 512
D_RW = 768
D_SSD = 768
IN_COLS = 6552
IN_PAD = 6656
RW0 = 1024
SS0 = 1024 + 2944
NEG = -60000.0
EPS = 1e-6


class _Q:
    def __init__(self, name, hw, sem):
        self.name = name
        self.hw = hw
        self.sem = sem
        self.count = 0
        self.waited = {}
        self.dsems = []
        self.dnext = 0


class K:
    def __init__(self, nc, es, n_dma_sems=24):
        self.nc = nc
        self.q = {}
        for name, hw in (("pe", nc.tensor), ("act", nc.scalar), ("dve", nc.vector),
                         ("pool", nc.gpsimd), ("sp", nc.sync)):
            sem = es.enter_context(nc.semaphore("s_" + name))
            self.q[name] = _Q(name, hw, sem)
        for qn in ("sp", "pool"):
            q = self.q[qn]
            for i in range(n_dma_sems):
                s = es.enter_context(nc.semaphore("d_%s_%d" % (qn, i)))
                q.dsems.append([s, 0])
        self.last_w = {}
        self.readers = {}
        self.n_wait = 0
        self.n_ins = 0
        nep = {"pe": 20, "act": 12, "dve": 12, "pool": 4, "sp": 2}
        for name, q in self.q.items():
            q.sems = [q.sem] + [es.enter_context(nc.semaphore("s_%s_e%d" % (name, i))) for i in range(1, nep[name])]
            q.epoch = 0

    def _wait(self, q, ev):
        sem, val = ev
        key = id(sem)
        if q.waited.get(key, 0) >= val:
            return
        q.hw.wait_ge(sem, val)
        q.waited[key] = val
        self.n_wait += 1

    def _deps(self, q, reads, writes, own_sem):
        evs = []
        for r in reads:
            e = self.last_w.get(r)
            if e is not None:
                evs.append(e)
            if isinstance(r, tuple) and r[0] == "ps":
                for e in self.readers.get(r, ()):
                    if e[0] is not own_sem:
                        evs.append(e)
        for w in writes:
            e = self.last_w.get(w)
            if e is not None:
                evs.append(e)
            for e in self.readers.get(w, ()):
                evs.append(e)
        for e in evs:
            if q.name == "pe" and e[0] is own_sem:
                continue
            self._wait(q, e)

    def _record(self, ev, reads, writes):
        for w in writes:
            self.last_w[w] = ev
            self.readers[w] = []
        for r in reads:
            if r in writes:
                continue
            lst = self.readers.setdefault(r, [])
            lst[:] = [x for x in lst if x[0] is not ev[0]]
            lst.append(ev)

    def op(self, qn, fn, reads=(), writes=()):
        q = self.q[qn]
        self._deps(q, reads, writes, q.sem)
        ins = fn(q.hw)
        ins.then_inc(q.sem, 1)
        q.count += 1
        self.n_ins += 1
        ev = (q.sem, q.count)
        self._record(ev, reads, writes)
        return ev

    def dma(self, qn, out, in_, reads=(), writes=(), fn=None, inc=16, slow=False):
        q = self.q[qn]
        slot = q.dsems[q.dnext]
        q.dnext = (q.dnext + 1) % len(q.dsems)
        sem = slot[0]
        if slot[1]:
            self._wait(q, (sem, slot[1]))
        self._deps(q, reads, writes, sem)
        if fn is None:
            if slow:
                ins = q.hw.dma_start(out=out, in_=in_, allow_slow_non_contiguous=True)
            else:
                ins = q.hw.dma_start(out=out, in_=in_)
        else:
            ins = fn(q.hw)
        ins.then_inc(sem, inc)
        slot[1] += inc
        self.n_ins += 1
        ev = (sem, slot[1])
        self._record(ev, reads, writes)
        return ev

    def barrier(self):
        evs = []
        for q in self.q.values():
            if q.count:
                evs.append((q.sem, q.count))
            for sem, val in q.dsems:
                if val:
                    evs.append((sem, val))
        for q in self.q.values():
            for e in evs:
                self._wait(q, e)
        self.last_w.clear()
        self.readers.clear()
        for q in self.q.values():
            if q.count > 12000 and q.epoch + 1 < len(q.sems):
                q.epoch += 1
                q.sem = q.sems[q.epoch]
                q.count = 0


def _vec_layout():
    off = {}
    n = 0

    def add(name, cols):
        nonlocal n
        off[name] = (n, cols)
        n += cols

    add("norm1_g", 16)
    add("norm2_g", 16)
    add("final_g", 16)
    add("lru_conv_w", 16)
    add("lru_conv_b", 4)
    add("lru_gate_b", 16)
    add("lru_lam", 8)
    add("rw_mu", 48)
    add("rw_w0", 12)
    add("rw_a0", 12)
    add("rw_kk", 6)
    add("rw_ka", 6)
    add("rw_rk", 6)
    add("rw_lng", 6)
    add("rw_lnb", 6)
    add("ss_conv_w", 56)
    add("ss_conv_b", 14)
    add("ss_d", 6)
    add("ss_ng", 6)
    add("ss_dtb", 1)
    add("ss_alog", 1)
    return off, n


VOFF, NV = _vec_layout()

RW_TILES = [(i * 128, 128) for i in range(18)] + [(2304, 96), (2400, 96), (2496, 96), (2592, 96),
                                                  (2688, 128), (2816, 128)]


def _pm(v):
    v = np.asarray(v, np.float32)
    return np.ascontiguousarray(v.reshape(-1, 128).T)


def _host_layout(inp, depth):
    f = np.float32
    out = {}
    vec = np.zeros((depth, 128, NV), f)

    def put(l, name, arr):
        o, c = VOFF[name]
        assert arr.shape == (128, c), (name, arr.shape, c)
        vec[l, :, o:o + c] = arr

    for l in range(depth):
        put(l, "norm1_g", _pm(inp["norm1_g"][l]))
        put(l, "norm2_g", _pm(inp["norm2_g"][l]))
        put(l, "final_g", _pm(inp["final_norm_g"]))
        put(l, "lru_conv_w", np.concatenate([_pm(inp["lru_conv_w"][l, s]) for s in range(4)], 1))
        put(l, "lru_conv_b", _pm(inp["lru_conv_b"][l]))
        put(l, "lru_gate_b", np.concatenate([_pm(inp["lru_gate_b"][l, d, g].reshape(-1))
                                             for d in range(2) for g in range(2)], 1))
        put(l, "lru_lam", np.concatenate([_pm(inp["lru_lambda"][l, d]) for d in range(2)], 1))
        mu = np.zeros((128, 48), f)
        for m in range(2):
            for ti, (r0, nr) in enumerate(RW_TILES):
                mu[:nr, m * 24 + ti] = inp["rwkv_mu"][l, m, r0:r0 + nr]
        put(l, "rw_mu", mu)
        put(l, "rw_w0", np.concatenate([_pm(inp["rwkv_w0"][l, d]) for d in range(2)], 1))
        put(l, "rw_a0", np.concatenate([_pm(inp["rwkv_a0"][l, d]) for d in range(2)], 1))
        put(l, "rw_kk", _pm(inp["rwkv_k_k"][l]))
        put(l, "rw_ka", _pm(inp["rwkv_k_a"][l]))
        put(l, "rw_rk", _pm(inp["rwkv_r_k"][l].reshape(-1)))
        put(l, "rw_lng", _pm(inp["rwkv_ln_g"][l]))
        put(l, "rw_lnb", _pm(inp["rwkv_ln_b"][l]))
        put(l, "ss_conv_w", np.concatenate([_pm(inp["ssd_conv_w"][l, s]) for s in range(4)], 1))
        put(l, "ss_conv_b", _pm(inp["ssd_conv_b"][l]))
        put(l, "ss_d", _pm(np.repeat(inp["ssd_d"][l], 64)))
        put(l, "ss_ng", _pm(inp["ssd_norm_g"][l]))
        dtb = np.zeros((128, 1), f)
        alog = np.zeros((128, 1), f)
        for base in (0, 32):
            dtb[base:base + 24, 0] = inp["ssd_dt_bias"][l].reshape(-1)
            alog[base:base + 24, 0] = inp["ssd_a_log"][l].reshape(-1)
        put(l, "ss_dtb", dtb)
        put(l, "ss_alog", alog)
    out["vec"] = vec

    def wblocks(w, ncolblk, colblk):
        dep, kk, nn = w.shape
        pad = ncolblk * colblk - nn
        if pad:
            w = np.concatenate([w, np.zeros((dep, kk, pad), f)], 2)
        w = w.reshape(dep, kk // 128, 128, ncolblk, colblk)
        return np.ascontiguousarray(w.transpose(0, 3, 2, 1, 4))

    out["w_in"] = wblocks(np.asarray(inp["w_in"][:depth], f), 13, 512)
    out["w_out"] = wblocks(np.asarray(inp["w_out"][:depth], f), 4, 512)
    out["w1"] = wblocks(np.asarray(inp["mlp_w1"][:depth], f), 16, 512)
    out["w2"] = wblocks(np.asarray(inp["mlp_w2"][:depth], f), 16, 128)
    gw = np.asarray(inp["lru_gate_w"][:depth], f)
    bd = np.zeros((depth, 128, 16, 128), f)
    for d in range(2):
        for g in range(2):
            for ct in range(4):
                j = (d * 2 + g) * 4 + ct
                for hh in range(2):
                    bd[:, hh * 64:(hh + 1) * 64, j, hh * 64:(hh + 1) * 64] = gw[:, d, g, ct * 2 + hh]
    out["lru_gw"] = bd
    up = np.zeros((depth, 128, 4, 768), f)
    up[:, :96, 0:2] = np.asarray(inp["rwkv_w_up"][:depth], f).transpose(0, 2, 1, 3)
    up[:, :96, 2:4] = np.asarray(inp["rwkv_a_up"][:depth], f).transpose(0, 2, 1, 3)
    out["rw_up"] = up
    out["rw_gup"] = np.ascontiguousarray(
        np.asarray(inp["rwkv_g_up"][:depth], f).reshape(depth, 2, 128, 768).transpose(0, 2, 1, 3))
    return out


def _consts(L):
    f = np.float32
    c = {}
    i = np.arange(128)
    c["ident"] = np.eye(128, dtype=f)
    c["tri"] = (i[:, None] <= i[None, :]).astype(f)
    c["trit"] = (i[:, None] >= i[None, :]).astype(f)
    c["negf"] = np.where(i[None, :] >= i[:, None], 0.0, NEG).astype(f)
    c["negb"] = np.where(i[None, :] <= i[:, None], 0.0, NEG).astype(f)
    c["ones"] = np.ones((128, 128), f)
    bd = np.zeros((128, 128), f)
    bd[:64, :64] = 1.0
    bd[64:, 64:] = 1.0
    c["bd64"] = bd
    t = np.arange(L)
    c["rstf"] = np.broadcast_to((t % 64 != 0).astype(f), (128, L)).copy()
    c["rstb"] = np.broadcast_to((t % 64 != 63).astype(f), (128, L)).copy()
    s = np.arange(64)[:, None]
    tt = np.arange(64)[None, :]
    c["rwmf"] = np.concatenate([(tt > s), (tt >= s)], 1).astype(f)
    c["rwmb"] = np.concatenate([(tt < s), (tt <= s)], 1).astype(f)
    c["rwnf"] = (tt < s).astype(f)
    c["rwnb"] = (tt > s).astype(f)
    return c


CONST_ORDER = ["ident", "tri", "trit", "negf", "negb", "ones", "bd64"]


class B:
    def __init__(self, L, depth, debug=(), ext_in=()):
        self.ext_in = set(ext_in)
        self.L = L
        self.depth = depth
        self.debug = set(debug)
        self.TT = min(512, L)
        self.NT = L // self.TT
        self.nc = nc = bass.Bass("TRN2", target_bir_lowering=False)
        self.es = ExitStack()
        self.k = K(nc, self.es)
        self.uid = 0
        self.flip = 0

        def din(name, shape, dt=F32):
            return nc.dram_tensor(name, list(shape), dt, kind="ExternalInput").ap()

        def dscr(name, shape, dt=F32):
            kind = "ExternalOutput" if name in self.debug else "Internal"
            if name in self.ext_in:
                kind = "ExternalInput"
            return nc.dram_tensor(name, list(shape), dt, kind=kind).ap()

        small = "small_w" in self.debug
        _din = din

        def din(name, shape, dt=F32):
            if small and name in ("w_in", "w_out", "w1", "w2"):
                shape = [1, 1, 1, 1, 1]
            return _din(name, shape, dt)

        self.xT = din("xT", [D, L])
        self.vec_d = din("vec", [depth, 128, NV])
        self.w_in = din("w_in", [depth, 13, 128, 16, 512])
        self.w_out = din("w_out", [depth, 4, 128, 16, 512])
        self.w1 = din("w1", [depth, 16, 128, 16, 512])
        self.w2 = din("w2", [depth, 16, 128, 64, 128])
        self.lru_gw = din("lru_gw", [depth, 128, 16, 128])
        self.rw_up = din("rw_up", [depth, 128, 4, 768])
        self.rw_gup = din("rw_gup", [depth, 128, 2, 768])
        self.cmat_d = din("cmat", [128, len(CONST_ORDER), 128])
        self.rst_d = din("rst", [128, 2, L])
        self.rwm_d = din("rwm", [64, 2, 128])
        self.rwn_d = din("rwn", [64, 2, 64])
        self.flags_d = din("flags", [128, 48])
        self.hsel_d = din("hsel", [24, 4])
        self.yT = nc.dram_tensor("yT", [D, L], F32, kind="ExternalOutput").ap()
        self.xres = dscr("xres", [D, L])
        self.pT = dscr("pT", [IN_PAD, L])
        self.ymix = dscr("ymix", [D, L], BF16)
        self.hT = dscr("hT", [DFF, L], BF16)
        self.dscr = dscr
        self.psum = [self.es.enter_context(nc.psum_tensor("ps%d" % i, [128, 1024], F32)) for i in range(4)]
        self.ps_rr = 0
        self.phase_es = None

    def sb(self, name, shape, dt):
        self.uid += 1
        return self.phase_es.enter_context(self.nc.sbuf_tensor("%s_%d" % (name, self.uid), list(shape), dt))

    def gsb(self, name, shape, dt):
        return self.es.enter_context(self.nc.sbuf_tensor("g_" + name, list(shape), dt))

    def begin(self):
        assert self.phase_es is None
        self.phase_es = ExitStack()

    def end(self):
        self.k.barrier()
        self.phase_es.close()
        self.phase_es = None

    def push(self):
        self.stack = getattr(self, "stack", [])
        self.stack.append(self.phase_es)
        self.phase_es = ExitStack()

    def pop(self):
        self.k.barrier()
        self.phase_es.close()
        self.phase_es = self.stack.pop()

    def bank(self, i=None):
        if i is None:
            pool = getattr(self, "ps_pool", None) or list(range(8))
            self.ps_rr = (self.ps_rr + 1) % len(pool)
            i = pool[self.ps_rr]
        t = self.psum[i // 2]
        return t[:, (i % 2) * 512:(i % 2) * 512 + 512], ("ps", i)

    def mm(self, out, lhsT, rhs, start, stop, reads, writes):
        return self.k.op("pe", lambda e: e.matmul(out, lhsT, rhs, start=start, stop=stop), reads, writes)

    def tr(self, out, in_, ident, reads, writes):
        return self.k.op("pe", lambda e: e.transpose(out, in_, ident), reads, writes)

    def act(self, out, in_, func, reads, writes, bias=0.0, scale=1.0, accum=None):
        def f(e):
            if accum is not None:
                return e.activation(out=out, in_=in_, func=func, bias=bias, scale=scale, accum_out=accum)
            return e.activation(out=out, in_=in_, func=func, bias=bias, scale=scale)
        return self.k.op("act", f, reads, writes)

    def tt(self, out, in0, in1, op, reads, writes, eng="dve"):
        return self.k.op(eng, lambda e: e.tensor_tensor(out=out, in0=in0, in1=in1, op=op), reads, writes)

    def ts(self, out, in0, s1, op0, reads, writes, s2=None, op1=None, eng="dve"):
        def f(e):
            if op1 is None:
                return e.tensor_scalar(out=out, in0=in0, scalar1=s1, scalar2=None, op0=op0)
            return e.tensor_scalar(out=out, in0=in0, scalar1=s1, scalar2=s2, op0=op0, op1=op1)
        return self.k.op(eng, f, reads, writes)

    def stt(self, out, in0, scalar, in1, op0, op1, reads, writes):
        return self.k.op("dve", lambda e: e.scalar_tensor_tensor(out=out, in0=in0, scalar=scalar, in1=in1,
                                                                 op0=op0, op1=op1), reads, writes)

    def copy(self, out, in_, reads, writes, eng=None):
        if eng is None:
            self.flip ^= 1
            eng = "act" if self.flip else "dve"
        if eng == "act":
            return self.act(out, in_, AF.Copy, reads, writes)
        return self.k.op(eng, lambda e: e.tensor_copy(out=out, in_=in_), reads, writes)

    def recip(self, out, in_, reads, writes):
        return self.k.op("dve", lambda e: e.reciprocal(out=out, in_=in_), reads, writes)

    def scan(self, out, d0, d1, init, reads, writes):
        return self.k.op("dve", lambda e: e.tensor_tensor_scan(out=out, data0=d0, data1=d1, initial=init,
                                                               op0=ALU.mult, op1=ALU.add), reads, writes)

    def ld(self, out, in_, reads, writes, slow=False):
        return self.k.dma("sp", out, in_, reads, writes, slow=slow)

    def st(self, out, in_, reads, writes, slow=False):
        return self.k.dma("sp", out, in_, reads, writes, slow=slow)

    def ldc(self, out, in_, reads, writes):
        return self.k.dma("pool", out, in_, reads, writes)

    def load_consts(self):
        nc = self.nc
        self.cm32 = self.gsb("cm32", [128, len(CONST_ORDER), 128], F32)
        self.cm16 = self.gsb("cm16", [128, len(CONST_ORDER), 128], BF16)
        self.ld(self.cm32[:], self.cmat_d, [], ["cm32"])
        self.ldc(self.cm16[:], self.cmat_d, [], ["cm16"])
        self.flags = self.gsb("flags", [128, 48], F32)
        self.ld(self.flags[:], self.flags_d, [], ["flags"])
        self.vec = self.gsb("vec", [128, NV], F32)
        self.onesD = self.gsb("onesD", [128, 128], BF16)
        self.ts(self.onesD[:], self.cm32[:, 5, :], 1.0 / D, ALU.mult, ["cm32"], ["onesD"])
        self.ones768 = self.gsb("ones768", [128, 128], BF16)
        self.ts(self.ones768[:], self.cm32[:, 5, :], 1.0 / 768, ALU.mult, ["cm32"], ["ones768"])
        self.bd64s = self.gsb("bd64s", [128, 128], F32)
        self.ts(self.bd64s[:], self.cm32[:, 6, :], 1.0 / 64, ALU.mult, ["cm32"], ["bd64s"])

    def C32(self, name):
        return self.cm32[:, CONST_ORDER.index(name), :]

    def C16(self, name):
        return self.cm16[:, CONST_ORDER.index(name), :]

    def V(self, name, col=0, n=1, p0=0, p1=128):
        o, c = VOFF[name]
        assert col + n <= c
        return self.vec[p0:p1, o + col:o + col + n]

    def load_vec(self, l):
        self.ld(self.vec[:], self.vec_d[l], [], ["vec"])

    def norm_phase(self, src, gname, uT=None, dst=None):
        L, TT, NT = self.L, self.TT, self.NT
        srcv = src.rearrange("(kc p) t -> p kc t", p=128)
        xts = [self.sb("xt", [128, KC, TT], F32) for _ in range(2)]
        sq = self.sb("sq", [128, KC, TT], BF16)
        sd = self.sb("sd", [128, TT], F32)
        rstd = self.sb("rstd", [128, TT], F32)
        for t in range(NT):
            xt = xts[t % 2]
            xk = ("xt", t % 2)
            cs = slice(t * TT, (t + 1) * TT)
            self.ld(xt[:], srcv[:, :, cs], [("xres", kc) for kc in range(KC)], [xk])
            self.act(sq[:], xt[:], AF.Square, [xk], ["sq"])
            ps, pk = self.bank()
            for kc in range(KC):
                self.mm(ps[:, :TT], self.onesD[:], sq[:, kc, :], kc == 0, kc == KC - 1, ["onesD", "sq"], [pk])
            self.act(sd[:], ps[:, :TT], AF.Sqrt, [pk], ["sd"], bias=EPS)
            self.recip(rstd[:], sd[:], ["sd"], ["rstd"])
            if uT is not None:
                for kc in range(KC):
                    self.stt(uT[:, kc, cs], xt[:, kc, :], self.V(gname, kc), rstd[:], ALU.mult, ALU.mult,
                             [xk, "vec", "rstd"], ["uT"])
            else:
                for kc in range(KC):
                    self.stt(xt[:, kc, :], xt[:, kc, :], self.V(gname, kc), rstd[:], ALU.mult, ALU.mult,
                             [xk, "vec", "rstd"], [xk])
                self.st(dst.rearrange("(kc p) t -> p kc t", p=128)[:, :, cs], xt[:], [xk], ["dst"])

    def wload(self, wbufs, i, wsrc):
        wb = wbufs[i % len(wbufs)]
        key = ("wb", i % len(wbufs))
        flat_o = wb[:].rearrange("p a b -> p (a b)").rearrange("p (x y) -> p x y", y=2048)
        flat_i = wsrc.rearrange("p a b -> p (a b)").rearrange("p (x y) -> p x y", y=2048)
        self.ldc(flat_o, flat_i, [], [key])
        return wb, key

    def proj_phase(self, uT, w_l, nblk, ncols, evac):
        L, TT, NT = self.L, self.TT, self.NT
        wbufs = [self.sb("wb", [128, KC, 512], BF16) for _ in range(3)]
        pend = {}
        for i in range(min(2, nblk)):
            pend[i] = self.wload(wbufs, i, w_l[i])
        for blk in range(nblk):
            wb, wk = pend.pop(blk)
            for sub in range(4):
                row0 = blk * 512 + sub * 128
                if row0 >= ncols:
                    continue
                for t in range(NT):
                    ps, pk = self.bank()
                    for kc in range(KC):
                        self.mm(ps[:, :TT], wb[:, kc, sub * 128:(sub + 1) * 128], uT[:, kc, t * TT:(t + 1) * TT],
                                kc == 0, kc == KC - 1, [wk, "uT"], [pk])
                    evac(ps, pk, row0, t)
            if blk + 2 < nblk:
                pend[blk + 2] = self.wload(wbufs, blk + 2, w_l[blk + 2])

    def inproj_phase(self, l):
        L, TT = self.L, self.TT
        self.begin()
        uT = self.sb("uT", [128, KC, L], BF16)
        self.hs = self.sb("hs", [128, 52, 4], F32)
        self.push()
        self.norm_phase(self.xres, "norm1_g", uT=uT)
        self.pop()
        stgs = [self.sb("stg", [128, L], F32) for _ in range(3)]
        st = {"i": 0}

        def evac(ps, pk, row0, t):
            i = st["i"] % 3
            sk = ("stg", i)
            self.copy(stgs[i][:, t * TT:(t + 1) * TT], ps[:, :TT], [pk], [sk])
            rb = row0 // 128
            if t == 0:
                self.copy(self.hs[:, rb, 0:2], stgs[i][:, 0:2], [sk], ["hs"], eng="dve")
            if t == self.NT - 1:
                self.copy(self.hs[:, rb, 2:3], stgs[i][:, L - 1:L], [sk], ["hs"], eng="dve")
                self.st(self.pT[row0:row0 + 128, :], stgs[i][:], [sk], [("pT", rb)])
                st["i"] += 1

        self.proj_phase(uT, self.w_in[l], 13, IN_COLS, evac)
        self.halo_exchange()
        self.end()

    def outproj_phase(self, l):
        L, TT = self.L, self.TT
        self.begin()
        yb = self.sb("yb", [128, KC, L], BF16)
        self.ld(yb[:], self.ymix.rearrange("(kc p) t -> p kc t", p=128), [("ymix", i) for i in range(KC)], ["uT"])
        self.resid_proj(yb, self.w_out[l], 4)
        self.end()

    def resid_proj(self, uT, w_l, nblk):
        L, TT = self.L, self.TT
        xrs = [self.sb("xr", [128, L], F32) for _ in range(3)]
        st = {"i": 0}

        def evac(ps, pk, row0, t):
            i = st["i"] % 3
            xk = ("xr", i)
            rb = row0 // 128
            if t == 0:
                self.ld(xrs[i][:], self.xres[row0:row0 + 128, :], [("xres", rb)], [xk])
            self.tt(xrs[i][:, t * TT:(t + 1) * TT], ps[:, :TT], xrs[i][:, t * TT:(t + 1) * TT], ALU.add,
                    [pk, xk], [xk])
            if t == self.NT - 1:
                self.st(self.xres[row0:row0 + 128, :], xrs[i][:], [xk], [("xres", rb)])
                st["i"] += 1

        self.proj_phase(uT, w_l, nblk, nblk * 512, evac)

    def mlp_phase(self, l):
        L, TT, NT = self.L, self.TT, self.NT
        self.begin()
        uT = self.sb("uT", [128, KC, L], BF16)
        self.push()
        self.norm_phase(self.xres, "norm2_g", uT=uT)
        self.pop()
        hst = [self.sb("hst", [128, L], BF16) for _ in range(3)]
        rl = [self.sb("rl", [128, TT], F32) for _ in range(2)]
        st = {"i": 0, "r": 0}

        def evac(ps, pk, row0, t):
            i = st["i"] % 3
            sk = ("hst", i)
            r = st["r"] % 2
            st["r"] += 1
            rk = ("rl", r)
            self.act(rl[r][:], ps[:, :TT], AF.Relu, [pk], [rk])
            self.tt(hst[i][:, t * TT:(t + 1) * TT], rl[r][:], rl[r][:], ALU.mult, [rk], [sk])
            if t == NT - 1:
                self.st(self.hT[row0:row0 + 128, :], hst[i][:], [sk], [("hT", row0 // 128)])
                st["i"] += 1

        self.proj_phase(uT, self.w1[l], 16, DFF, evac)
        self.end()
        self.begin()
        ht = self.sb("ht", [128, 64, TT], BF16)
        wbufs = [self.sb("w2b", [128, 64, 128], BF16) for _ in range(3)]
        xrs = [self.sb("xr2", [128, TT], F32) for _ in range(3)]
        n = 0
        pend = {}
        total = NT * 16
        for i in range(min(2, total)):
            pend[i] = self.wload(wbufs, i, self.w2[l, i % 16])
        hv = self.hT.rearrange("(fc p) t -> p fc t", p=128)
        for t in range(NT):
            cs = slice(t * TT, (t + 1) * TT)
            self.ld(ht[:], hv[:, :, cs], [("hT", i) for i in range(64)], ["ht"])
            for db in range(16):
                wb, wk = pend.pop(n)
                xi = n % 3
                xk = ("xr2", xi)
                self.ld(xrs[xi][:], self.xres[db * 128:(db + 1) * 128, cs], [("xres", db)], [xk])
                ps, pk = self.bank()
                for fc in range(64):
                    self.mm(ps[:, :TT], wb[:, fc, :], ht[:, fc, :], fc == 0, fc == 63, [wk, "ht"], [pk])
                self.tt(xrs[xi][:], ps[:, :TT], xrs[xi][:], ALU.add, [pk, xk], [xk])
                self.st(self.xres[db * 128:(db + 1) * 128, cs], xrs[xi][:], [xk], [("xres", db)])
                if n + 2 < total:
                    pend[n + 2] = self.wload(wbufs, n + 2, self.w2[l, (n + 2) % 16])
                n += 1
        self.end()

    def final_phase(self):
        self.begin()
        self.norm_phase(self.xres, "final_g", dst=self.yT)
        self.end()

    def copy_in(self):
        self.begin()
        L = self.L
        tb = [self.sb("cin", [128, L], F32) for _ in range(2)]
        for kc in range(KC):
            i = kc % 2
            self.ld(tb[i][:], self.xT[kc * 128:(kc + 1) * 128, :], [], [("cin", i)])
            self.st(self.xres[kc * 128:(kc + 1) * 128, :], tb[i][:], [("cin", i)], [("xres", kc)])
        self.end()


def build_program(L, depth, debug=(), stub=False):
    b = B(L, depth, debug)
    b.begin()
    b.load_consts()
    b.end()
    b.copy_in()
    for l in range(depth):
        b.begin()
        b.load_vec(l)
        b.end()
        b.inproj_phase(l)
        if stub:
            b.begin()
            tb = b.sb("stub", [128, L], F32)
            tb2 = b.sb("stub2", [128, L], BF16)
            for kc in range(KC):
                b.ld(tb[:], b.pT[kc * 128:(kc + 1) * 128, :], [], ["tb"])
                b.copy(tb2[:], tb[:], ["tb"], ["tb2"])
                b.st(b.ymix[kc * 128:(kc + 1) * 128, :], tb2[:], ["tb2"], [])
            b.end()
        else:
            b.mixers(l)
        b.outproj_phase(l)
        b.mlp_phase(l)
    b.final_phase()
    b.k.barrier()
    b.es.close()
    return b


def make_in_maps(inputs, L, depth):
    xp = np.asarray(inputs["x_prompt"], np.float32)[0]
    xs = np.asarray(inputs["x_sample"], np.float32)
    assert xp.shape[0] == 4 * L and xs.shape[1] == L
    segs = [xp[i * L:(i + 1) * L] for i in range(4)] + [xs[i] for i in range(4)]
    lay = _host_layout(inputs, depth)
    cs = _consts(L)
    cmat = np.ascontiguousarray(np.stack([cs[n] for n in CONST_ORDER], 1))
    rst = np.ascontiguousarray(np.stack([cs["rstf"], cs["rstb"]], 1))
    rwm = np.ascontiguousarray(np.stack([cs["rwmf"], cs["rwmb"]], 1))
    rwn = np.ascontiguousarray(np.stack([cs["rwnf"], cs["rwnb"]], 1))
    link_f = np.zeros(8, np.float32)
    link_f[0:3] = 1.0
    link_b = np.zeros(8, np.float32)
    link_b[1:4] = 1.0
    in_maps = []
    for c in range(NCORE):
        flags = np.zeros((128, 48), np.float32)
        flags[:, 0:8] = link_f
        flags[:, 8:16] = link_b
        flags[:, 16 + c] = 1.0
        if c >= 1 and link_f[c - 1]:
            flags[:, 24 + c - 1] = 1.0
        if c + 1 < NCORE and link_f[c]:
            flags[:, 32 + c + 1] = 1.0
        hsel = np.zeros((24, 4), np.float32)
        if c >= 1 and link_f[c - 1]:
            hsel[(c - 1) * 3 + 2, 0] = 1.0
        if c + 1 < NCORE and link_f[c]:
            hsel[(c + 1) * 3 + 0, 1] = 1.0
            hsel[(c + 1) * 3 + 1, 2] = 1.0
        m = {"xT": np.ascontiguousarray(segs[c].T), "vec": lay["vec"], "w_in": lay["w_in"], "w_out": lay["w_out"],
             "w1": lay["w1"], "w2": lay["w2"], "lru_gw": lay["lru_gw"], "rw_up": lay["rw_up"],
             "rw_gup": lay["rw_gup"], "cmat": cmat, "rst": rst, "rwm": rwm, "rwn": rwn, "flags": flags,
             "hsel": hsel}
        in_maps.append(m)
    return in_maps


def run(inputs, L, depth, debug=(), stub=False):
    b = build_program(L, depth, debug, stub)
    in_maps = make_in_maps(inputs, L, depth)
    res = run_bass_kernel_spmd(b.nc, in_maps, core_ids=list(range(NCORE)))
    return res.results


def kernel(**inputs):
    L = 2048
    depth = 4
    res = run(inputs, L, depth)
    ys = [np.ascontiguousarray(r["yT"].T) for r in res]
    y_prompt = np.concatenate(ys[0:4], 0)[None].astype(np.float32)
    y_sample = np.stack(ys[4:8], 0).astype(np.float32)
    return (y_prompt, y_sample)


def _add_methods(cls):
    def deco(f):
        setattr(cls, f.__name__, f)
        return f
    return deco


@_add_methods(B)
def allgather(self, src, dst, rkeys, wkeys):
    self.k.dma("pool", None, None, rkeys, wkeys, inc=1,
               fn=lambda e: e.collective_compute("AllGather", ALU.bypass, replica_groups=[list(range(NCORE))],
                                                 ins=[src], outs=[dst]))


@_add_methods(B)
def halo_exchange(self):
    if not hasattr(self, "halo_in"):
        self.halo_in = self.dscr("halo_in", [128, 208])
        self.halo_all = self.dscr("halo_all", [NCORE * 128, 208])
        self.halo_d = self.dscr("halo_d", [IN_PAD, 4])
    self.st(self.halo_in, self.hs[:].rearrange("p a b -> p (a b)"), ["hs"], ["halo_in"])
    self.allgather(self.halo_in, self.halo_all, ["halo_in"], ["halo_all"])
    G = self.sb("haloG", [128, NCORE, 52, 4], F32)
    self.ld(G[:].rearrange("p r a b -> p r (a b)"), self.halo_all.rearrange("(r p) x -> p r x", p=128),
            ["halo_all"], ["haloG"])
    HL = self.sb("HL", [128, 52, 4], F32)
    self.k.op("dve", lambda e: e.memset(HL[:], 0.0), [], ["HL"])
    for r in range(NCORE):
        self.stt(HL[:, :, 0:1], G[:, r, :, 2:3], self.flags[:, 24 + r:25 + r], HL[:, :, 0:1], ALU.mult, ALU.add,
                 ["haloG", "flags", "HL"], ["HL"])
        self.stt(HL[:, :, 1:3], G[:, r, :, 0:2], self.flags[:, 32 + r:33 + r], HL[:, :, 1:3], ALU.mult, ALU.add,
                 ["haloG", "flags", "HL"], ["HL"])
    self.st(self.halo_d.rearrange("(rb p) c -> p rb c", p=128), HL[:], ["HL"], ["halo_d"], slow=True)


@_add_methods(B)
def load_ext(self, xe, row0, nr, key):
    L = self.L
    rbs = sorted(set([row0 // 128, (row0 + nr - 1) // 128]))
    self.ld(xe[0:nr, 1:L + 1], self.pT[row0:row0 + nr, :], [("pT", rb) for rb in rbs], [key])
    self.ld(xe[0:nr, 0:1], self.halo_d[row0:row0 + nr, 0:1], ["halo_d"], [key], slow=True)
    self.ld(xe[0:nr, L + 1:L + 3], self.halo_d[row0:row0 + nr, 1:3], ["halo_d"], [key], slow=True)


@_add_methods(B)
def state_chain(self, Tj, Fj, link0, order, HIN, hkey, tmp):
    g, g2 = tmp
    self.k.op("dve", lambda e: e.memset(g, 0.0), [], ["chain_g"])
    self.k.op("dve", lambda e: e.memset(HIN, 0.0), [], [hkey])
    for j in order:
        self.stt(HIN, g, self.flags[:, 16 + j:17 + j], HIN, ALU.mult, ALU.add, ["chain_g", "flags", hkey], [hkey])
        self.tt(g2, g, Tj(j), ALU.mult, ["chain_g", "stG"], ["chain_g2"])
        self.tt(g2, g2, Fj(j), ALU.add, ["chain_g2", "stG"], ["chain_g2"])
        self.ts(g, g2, self.flags[:, link0 + j:link0 + j + 1], ALU.mult, ["chain_g2", "flags"], ["chain_g"])


@_add_methods(B)
def lru_phase(self, l):
    L, TT, NT = self.L, self.TT, self.NT
    if not hasattr(self, "lru_scr"):
        self.lru_scr = self.dscr("lru_scr", [16, 128, L])
        self.lru_st_in = self.dscr("lru_st_in", [128, 16])
        self.lru_st_all = self.dscr("lru_st_all", [NCORE * 128, 16])
    self.begin()
    gw = self.sb("gw", [128, 16, 128], BF16)
    for j in range(4):
        self.ldc(gw[:, j * 4:(j + 1) * 4, :], self.lru_gw[l][:, j * 4:(j + 1) * 4, :], [], ["gw"])
    cd = self.sb("cd", [128, 8], F32)
    self.act(cd[:], self.V("lru_lam", 0, 8), AF.Exp, ["vec"], ["cd"], scale=-1.0)
    self.act(cd[:], cd[:], AF.Ln, ["cd"], ["cd"], bias=1.0)
    self.ts(cd[:], cd[:], -8.0, ALU.mult, ["cd"], ["cd"])
    cd2 = self.sb("cd2", [128, 8], F32)
    self.ts(cd2[:], cd[:], 2.0, ALU.mult, ["cd"], ["cd2"])
    t3 = self.sb("t3", [128, L], F32)
    TF = self.sb("TF", [128, 16], F32)
    xe = self.sb("xe", [128, L + 3], F32)
    xc = self.sb("xc", [128, L], F32)
    xcb = self.sb("xcb", [128, L], BF16)
    rr = self.sb("rr", [128, L], F32)
    ii = self.sb("ii", [128, L], F32)
    aa = [self.sb("aa", [128, L], F32) for _ in range(2)]
    bx = [self.sb("bx", [128, L], F32) for _ in range(2)]
    t1 = self.sb("t1", [128, L], F32)
    h0 = self.sb("h0", [128, L], F32)
    sm = self.sb("sm", [128, 4], F32)
    n = 0
    for ct in range(4):
        self.load_ext(xe, D_LRU + ct * 128, 128, "xe")
        cw = lambda s_: self.V("lru_conv_w", s_ * 4 + ct)
        self.ts(xc[:], xe[:, 0:L], cw(0), ALU.mult, ["xe", "vec"], ["xc"], s2=self.V("lru_conv_b", ct), op1=ALU.add)
        for s_ in range(1, 4):
            self.stt(xc[:], xe[:, s_:s_ + L], cw(s_), xc[:], ALU.mult, ALU.add, ["xe", "vec", "xc"], ["xc"])
        self.copy(xcb[:], xc[:], ["xc"], ["xcb"], eng="act")
        for d in range(2):
            a, b_ = aa[n % 2], bx[n % 2]
            ak, bk = ("aa", n % 2), ("bx", n % 2)
            n += 1
            for g in range(2):
                dst, dk = (rr, "rr") if g == 0 else (ii, "ii")
                j = (d * 2 + g) * 4 + ct
                for t in range(NT):
                    ps, pk = self.bank()
                    self.mm(ps[:, :TT], gw[:, j, :], xcb[:, t * TT:(t + 1) * TT], True, True, ["gw", "xcb"], [pk])
                    self.act(dst[:, t * TT:(t + 1) * TT], ps[:, :TT], AF.Sigmoid, [pk, "vec"], [dk],
                             bias=self.V("lru_gate_b", j))
            c = cd[:, d * 4 + ct:d * 4 + ct + 1]
            self.act(a[:], rr[:], AF.Exp, ["rr", "cd"], [ak], scale=c)
            self.tt(t1[:], a[:], a[:], ALU.mult, [ak], ["t1"])
            self.ts(t1[:], t1[:], -1.0, ALU.mult, ["t1"], ["t1"], s2=1.0, op1=ALU.add)
            self.ts(h0[:], rr[:], cd2[:, d * 4 + ct:d * 4 + ct + 1], ALU.mult, ["rr", "cd2"], ["h0"])
            self.ts(t3[:], h0[:], -0.5, ALU.mult, ["h0"], ["t3"], s2=-1.0, op1=ALU.add)
            self.tt(t3[:], t3[:], h0[:], ALU.mult, ["t3", "h0"], ["t3"])
            self.tt(t1[:], t1[:], t3[:], ALU.max, ["t1", "t3"], ["t1"])
            self.ts(t1[:], t1[:], 1e-30, ALU.max, ["t1"], ["t1"])
            self.act(t1[:], t1[:], AF.Sqrt, ["t1"], ["t1"])
            self.tt(b_[:], ii[:], xc[:], ALU.mult, ["ii", "xc"], [bk])
            self.tt(b_[:], b_[:], t1[:], ALU.mult, [bk, "t1"], [bk])
            self.k.op("dve", lambda e: e.reduce_sum(out=sm[:, 0:1], in_=rr[:], axis=mybir.AxisListType.X),
                      ["rr"], ["sm"])
            self.act(TF[:, (2 * d) * 4 + ct:(2 * d) * 4 + ct + 1], sm[:, 0:1], AF.Exp, ["sm", "cd"], ["TF"], scale=c)
            if d == 0:
                self.scan(h0[:], a[:], b_[:], 0.0, [ak, bk], ["h0"])
                self.copy(TF[:, 4 + ct:5 + ct], h0[:, L - 1:L], ["h0"], ["TF"], eng="dve")
            else:
                self.scan(h0[:, ::-1], a[:, ::-1], b_[:, ::-1], 0.0, [ak, bk], ["h0"])
                self.copy(TF[:, 12 + ct:13 + ct], h0[:, 0:1], ["h0"], ["TF"], eng="dve")
            self.st(self.lru_scr[(ct * 2 + d) * 2], a[:], [ak], [("lscr", (ct * 2 + d) * 2)])
            self.st(self.lru_scr[(ct * 2 + d) * 2 + 1], b_[:], [bk], [("lscr", (ct * 2 + d) * 2 + 1)])
    self.st(self.lru_st_in, TF[:], ["TF"], ["lst_in"])
    self.allgather(self.lru_st_in, self.lru_st_all, ["lst_in"], ["lst_all"])
    G = self.sb("stG", [128, NCORE, 16], F32)
    self.ld(G[:], self.lru_st_all.rearrange("(r p) x -> p r x", p=128), ["lst_all"], ["stG"])
    HIN = self.sb("HIN", [128, 8], F32)
    tmp = self.sb("ctmp", [128, 8], F32)
    self.state_chain(lambda j: G[:, j, 0:4], lambda j: G[:, j, 4:8], 0, list(range(NCORE)), HIN[:, 0:4], "HINf",
                     (tmp[:, 0:4], tmp[:, 4:8]))
    self.state_chain(lambda j: G[:, j, 8:12], lambda j: G[:, j, 12:16], 8, list(range(NCORE - 1, -1, -1)),
                     HIN[:, 4:8], "HINb", (tmp[:, 0:4], tmp[:, 4:8]))
    gb = xc
    yb = self.sb("ylru", [128, L], BF16)
    for ct in range(4):
        self.ld(gb[:], self.pT[ct * 128:(ct + 1) * 128, :], [("pT", ct)], ["xc"])
        self.act(gb[:], gb[:], AF.Gelu, ["xc"], ["xc"])
        for d in range(2):
            a, b_ = aa[d], bx[d]
            ak, bk = ("aa", d), ("bx", d)
            self.ld(a[:], self.lru_scr[(ct * 2 + d) * 2], [("lscr", (ct * 2 + d) * 2)], [ak])
            self.ld(b_[:], self.lru_scr[(ct * 2 + d) * 2 + 1], [("lscr", (ct * 2 + d) * 2 + 1)], [bk])
            init = HIN[:, d * 4 + ct:d * 4 + ct + 1]
            if d == 0:
                self.scan(h0[:], a[:], b_[:], init, [ak, bk, "HINf"], ["h0"])
            else:
                self.scan(t1[:, ::-1], a[:, ::-1], b_[:, ::-1], init, [ak, bk, "HINb"], ["t1"])
        self.tt(h0[:], h0[:], t1[:], ALU.add, ["h0", "t1"], ["h0"])
        self.tt(yb[:], h0[:], gb[:], ALU.mult, ["h0", "xc"], ["ylru"])
        self.st(self.ymix[ct * 128:(ct + 1) * 128, :], yb[:], ["ylru"], [("ymix", ct)])
    self.end()


@_add_methods(B)
def mixers(self, l):
    self.lru_phase(l)
    if "no_rwkv" not in self.debug:
        self.rwkv_phase(l)
    if "no_ssd" not in self.debug:
        self.ssd_phase(l)


@_add_methods(B)
def ssd_phase(self, l):
    L, TT, NT = self.L, self.TT, self.NT
    NCH = L // 128
    CPT = TT // 128
    nc = self.nc
    if not hasattr(self, "ssd_S"):
        self.ssd_S = self.dscr("ssd_S", [2, NCH, 128, 768])
        self.ssd_xs = self.dscr("ssd_xs", [768, L])
        self.ssd_HS = self.dscr("ssd_HS", [NCH, 128, 2 * 768], BF16)
        self.ssd_st_in = self.dscr("ssd_st_in", [128, 1568])
        self.ssd_st_all = self.dscr("ssd_st_all", [NCORE * 128, 1568])
    X0 = SS0 + 768
    self.begin()
    BT = self.sb("BT", [128, 4, L], BF16)
    CT = self.sb("CT", [128, 4, L], BF16)
    xtok = self.sb("xtok", [128, NCH, 768], BF16)
    dtok = self.sb("dtok", [128, NCH, 64], F32)
    acum = self.sb("acum", [128, NCH, 72], F32)
    ST = self.sb("ST", [128, 1568], F32)
    self.push()
    ah = self.sb("ah", [128, 1], F32)
    self.act(ah[:], self.V("ss_alog"), AF.Exp, ["vec"], ["ah"])
    self.ts(ah[:], ah[:], -1.0, ALU.mult, ["ah"], ["ah"])
    dtx = self.sb("dtx", [64, L], F32)
    self.k.op("dve", lambda e: e.memset(dtx[:], 0.0), [], ["dtx"])
    rbs = [(X0 + 1792) // 128, (X0 + 1792 + 23) // 128]
    for base in (0, 32):
        self.ld(dtx[base:base + 24, :], self.pT[X0 + 1792:X0 + 1816, :], [("pT", rb) for rb in rbs], ["dtx"])
    self.act(dtx[:], dtx[:], AF.Exp, ["dtx", "vec"], ["dtx"], bias=self.V("ss_dtb", p1=64))
    self.act(dtx[:], dtx[:], AF.Ln, ["dtx"], ["dtx"], bias=1.0)
    self.ts(dtx[32:64, :], dtx[32:64, :], ah[32:64, 0:1], ALU.mult, ["dtx", "ah"], ["dtx"])
    XF = self.sb("XF", [128, 6, L], BF16)
    xe = self.sb("xe", [128, L + 3], F32)
    xc = self.sb("xc", [128, L], F32)
    xs32 = [self.sb("xs32", [128, L], F32) for _ in range(2)]
    for i in range(14):
        self.load_ext(xe, X0 + i * 128, 128, "xe")
        cw = lambda s_: self.V("ss_conv_w", s_ * 14 + i)
        self.ts(xc[:], xe[:, 0:L], cw(0), ALU.mult, ["xe", "vec"], ["xc"], s2=self.V("ss_conv_b", i), op1=ALU.add)
        for s_ in range(1, 4):
            self.stt(xc[:], xe[:, s_:s_ + L], cw(s_), xc[:], ALU.mult, ALU.add, ["xe", "vec", "xc"], ["xc"])
        if i < 6:
            dstb, dk = XF[:, i, :], ("XF", i)
        elif i < 10:
            dstb, dk = BT[:, i - 6, :], ("BT", i - 6)
        else:
            dstb, dk = CT[:, i - 10, :], ("CT", i - 10)
        self.act(dstb, xc[:], AF.Silu, ["xc"], [dk])
        if i < 6:
            xk = ("xs32", i % 2)
            self.act(xs32[i % 2][:], xc[:], AF.Silu, ["xc"], [xk])
            self.st(self.ssd_xs[i * 128:(i + 1) * 128, :], xs32[i % 2][:], [xk], [("ssd_xs", i)])
    btok = self.sb("btok", [128, 512], BF16)
    wgt = self.sb("wgt", [128, 24], F32)
    xw = [self.sb("xw", [128, 12, 64], BF16) for _ in range(2)]
    Ssb = [self.sb("Ssb", [128, 768], F32) for _ in range(2)]
    hF = ST[:, 0:768]
    FB = ST[:, 768:1536]
    Pf = ST[:, 1536:1548]
    Pb = ST[:, 1548:1560]
    self.k.op("dve", lambda e: e.memset(ST[:], 0.0), [], ["ST"])
    self.k.op("dve", lambda e: e.memset(ST[:, 1536:1560], 1.0), ["ST"], ["ST"])
    stmp = self.sb("stmp", [128, 768], F32)
    h3 = lambda ap: ap.rearrange("p (h x) -> p h x", x=64)
    bc = lambda ap: ap.unsqueeze(2).to_broadcast([128, 12, 64])
    for q in range(NCH):
        cs = slice(q * 128, (q + 1) * 128)
        ps, pk = self.bank()
        psb = ps.bitcast(BF16)
        for i in range(6):
            self.tr(psb[:, i * 128:(i + 1) * 128], XF[:, i, cs], self.C16("ident"), [("XF", i), "cm16"], [pk])
        self.copy(xtok[:, q, :], psb[:, 0:768], [pk], [("xtok", q)])
        ps, pk = self.bank()
        psb = ps.bitcast(BF16)
        for g in range(4):
            self.tr(psb[:, g * 128:(g + 1) * 128], BT[:, g, cs], self.C16("ident"), [("BT", g), "cm16"], [pk])
        self.copy(btok[:], psb[:, 0:512], [pk], ["btok"])
        ps, pk = self.bank()
        self.tr(ps[:, 0:64], dtx[:, cs], self.C32("ident")[0:64, 0:64], ["dtx", "cm32"], [pk])
        self.copy(dtok[:, q, :], ps[:, 0:64], [pk], [("dtok", q)])
        ps, pk = self.bank()
        self.mm(ps[:, 0:12], self.C32("tri"), dtok[:, q, 32:44], True, True, ["cm32", ("dtok", q)], [pk])
        self.mm(ps[:, 12:24], self.C32("trit"), dtok[:, q, 44:56], True, True, ["cm32", ("dtok", q)], [pk])
        self.mm(ps[:, 24:48], self.C32("ones"), dtok[:, q, 32:56], True, True, ["cm32", ("dtok", q)], [pk])
        self.copy(acum[:, q, 0:48], ps[:, 0:48], [pk], [("acum", q)], eng="dve")
        self.act(acum[:, q, 48:72], acum[:, q, 24:48], AF.Exp, [("acum", q)], [("acum", q)])
        self.tt(wgt[:], acum[:, q, 24:48], acum[:, q, 0:24], ALU.subtract, [("acum", q)], ["wgt"])
        self.act(wgt[:], wgt[:], AF.Exp, ["wgt"], ["wgt"])
        self.tt(wgt[:], wgt[:], dtok[:, q, 0:24], ALU.mult, ["wgt", ("dtok", q)], ["wgt"])
        for d in range(2):
            self.tt(xw[d][:], h3(xtok[:, q, :]), bc(wgt[:, d * 12:(d + 1) * 12]), ALU.mult,
                    [("xtok", q), "wgt"], [("xw", d)])
            psA, pkA = self.bank()
            psB, pkB = self.bank()
            for g in range(4):
                pp, kk_ = (psA, pkA) if g < 2 else (psB, pkB)
                self.mm(pp[:, (g % 2) * 192:(g % 2) * 192 + 192], btok[:, g * 128:(g + 1) * 128],
                        xw[d][:, g * 3:(g + 1) * 3, :].rearrange("p a b -> p (a b)"), True, True,
                        ["btok", ("xw", d)], [kk_])
            S = Ssb[d]
            sk = ("Ssb", d)
            self.copy(S[:, 0:384], psA[:, 0:384], [pkA], [sk], eng="act")
            self.copy(S[:, 384:768], psB[:, 0:384], [pkB], [sk], eng="act")
            self.st(self.ssd_S[d, q], S[:], [sk], [("ssd_S", d, q)])
            et = acum[:, q, 48 + d * 12:48 + (d + 1) * 12]
            if d == 0:
                self.tt(h3(hF), h3(hF), bc(et), ALU.mult, ["ST", ("acum", q)], ["ST"])
                self.tt(hF, hF, S[:], ALU.add, ["ST", sk], ["ST"])
                self.tt(Pf, Pf, et, ALU.mult, ["ST", ("acum", q)], ["ST"])
            else:
                self.tt(h3(stmp[:]), h3(S[:]), bc(Pb), ALU.mult, [sk, "ST"], ["stmp"])
                self.tt(FB, FB, stmp[:], ALU.add, ["ST", "stmp"], ["ST"])
                self.tt(Pb, Pb, et, ALU.mult, ["ST", ("acum", q)], ["ST"])
    self.st(self.ssd_st_in, ST[:], ["ST"], ["sst_in"])
    self.allgather(self.ssd_st_in, self.ssd_st_all, ["sst_in"], ["sst_all"])
    self.pop()
    self.push()
    G = self.sb("stG", [128, NCORE, 1568], F32)
    self.ld(G[:], self.ssd_st_all.rearrange("(r p) x -> p r x", p=128), ["sst_all"], ["stG"])
    HINf = self.sb("HINf", [128, 768], F32)
    HINb = self.sb("HINb", [128, 768], F32)
    cg = self.sb("cg", [128, 768], F32)
    cg2 = self.sb("cg2", [128, 768], F32)
    self.state_chain(lambda j: bc(G[:, j, 1536:1548]), lambda j: h3(G[:, j, 0:768]), 0, list(range(NCORE)),
                     h3(HINf[:]), "HINf", (h3(cg[:]), h3(cg2[:])))
    self.state_chain(lambda j: bc(G[:, j, 1548:1560]), lambda j: h3(G[:, j, 768:1536]), 8,
                     list(range(NCORE - 1, -1, -1)), h3(HINb[:]), "HINb", (h3(cg[:]), h3(cg2[:])))
    self.copy(ST[:, 0:768], HINf[:], ["HINf"], ["ST"], eng="dve")
    self.copy(ST[:, 768:1536], HINb[:], ["HINb"], ["ST"], eng="dve")
    self.pop()
    if "ssd_A" in self.debug:
        self.end()
        return
    self.push()
    HSt = [self.sb("HSt", [128, 768], BF16) for _ in range(2)]
    hsl = [self.sb("hsl", [128, 2, 768], BF16) for _ in range(2)]
    Sld = [self.sb("Sld", [128, 768], F32) for _ in range(2)]
    n = 0
    for d in range(2):
        h = ST[:, d * 768:(d + 1) * 768]
        order = range(NCH) if d == 0 else range(NCH - 1, -1, -1)
        for q in order:
            self.copy(HSt[n % 2][:], h, ["ST"], [("HSt", n % 2)], eng="act")
            self.st(self.ssd_HS[q, :, d * 768:(d + 1) * 768], HSt[n % 2][:], [("HSt", n % 2)], [("ssd_HS", q, d)])
            S = Sld[n % 2]
            sk = ("Sld", n % 2)
            n += 1
            self.ld(S[:], self.ssd_S[d, q], [("ssd_S", d, q)], [sk])
            et = acum[:, q, 48 + d * 12:48 + (d + 1) * 12]
            self.tt(h3(h), h3(h), bc(et), ALU.mult, ["ST", ("acum", q)], ["ST"])
            self.tt(h, h, S[:], ALU.add, ["ST", sk], ["ST"])
    self.ps_pool = [0, 1, 2, 3, 4, 5]
    dab = self.sb("dab", [128, 24, 128], F32)
    cbm = self.sb("cbm", [128, 8, 128], F32)
    seg = [self.sb("seg", [128, 128], F32) for _ in range(3)]
    EE = [self.sb("EE", [128, 128], F32) for _ in range(3)]
    EA = [self.sb("EA", [128, 128], F32) for _ in range(3)]
    M = self.sb("M", [128, 12, 2, 128], BF16)
    Cs = self.sb("Cs", [128, 12, 2, 128], BF16)
    xz = [self.sb("xz", [128, 12, 128], BF16) for _ in range(2)]
    hz = [self.sb("hz", [128, 2, 12, 128], BF16) for _ in range(2)]
    for zi in range(2):
        self.k.op("pool", lambda e, zi=zi: e.memset(xz[zi][:], 0.0), [], [("xz", zi)])
        self.k.op("pool", lambda e, zi=zi: e.memset(hz[zi][:], 0.0), [], [("xz", zi)])
    ysb = self.sb("ysb", [128, 6, TT], F32)
    zt = self.sb("zt", [128, 6, TT], F32)
    xst = [self.sb("xst", [128, 6, 128], F32) for _ in range(2)]
    sqb = self.sb("sqb", [128, 6, TT], BF16)
    sd = self.sb("sd2", [128, TT], F32)
    yo = self.sb("yo", [128, 6, TT], BF16)
    zv = self.pT[SS0:SS0 + 768, :].rearrange("(a p) t -> p a t", p=128)
    xsv = self.ssd_xs.rearrange("(a p) t -> p a t", p=128)
    zrb = sorted(set(range(SS0 // 128, (SS0 + 767) // 128 + 1)))
    r = 0
    for q in range(NCH):
        cs = slice(q * 128, (q + 1) * 128)
        tcol = (q % CPT) * 128
        xs_t = xst[q % 2]
        xsk = ("xst", q % 2)
        self.ld(xs_t[:], xsv[:, :, cs], [("ssd_xs", i) for i in range(6)], [xsk])
        ps, pk = self.bank()
        for g in range(4):
            self.mm(ps[:, g * 128:(g + 1) * 128], BT[:, g, cs], CT[:, g, cs], True, True, [("BT", g), ("CT", g)], [pk])
        for g in range(4):
            self.tt(cbm[:, g * 2, :], ps[:, g * 128:(g + 1) * 128], self.C32("tri"), ALU.mult, [pk, "cm32"], ["cbm"])
            self.tt(cbm[:, g * 2 + 1, :], ps[:, g * 128:(g + 1) * 128], self.C32("trit"), ALU.mult, [pk, "cm32"],
                    ["cbm"])
        self.k.op("dve", lambda e: e.tensor_copy(out=dab[:], in_=dtok[:, q, 32:56].unsqueeze(2).to_broadcast(
            [128, 24, 128])), [("dtok", q)], ["dab"])
        psy0, pky0 = self.bank(6)
        psy1, pky1 = self.bank(7)
        for h in range(12):
            g = h // 3
            for d in range(2):
                hd = d * 12 + h
                if hd % 4 == 0:
                    psr, pkr = self.bank()
                reg = psr[:, (hd % 4) * 128:(hd % 4) * 128 + 128]
                self.mm(reg, dab[:, hd, :], self.C32("tri" if d == 0 else "trit"), True, True, ["dab", "cm32"], [pkr])
                i3 = r % 3
                r += 1
                self.act(EA[i3][:], reg, AF.Exp, [pkr], [("EA", i3)])
                self.ts(seg[i3][:], reg, acum[:, q, hd:hd + 1], ALU.subtract, [pkr, ("acum", q)], [("seg", i3)],
                        s2=0.0, op1=ALU.min)
                self.act(EE[i3][:], seg[i3][:], AF.Exp, [("seg", i3)], [("EE", i3)])
                self.stt(M[:, h, d, :], EE[i3][:], dtok[:, q, hd:hd + 1], cbm[:, g * 2 + d, :], ALU.mult, ALU.mult,
                         [("EE", i3), ("dtok", q), "cbm"], [("M", h)])
                self.tt(Cs[:, h, d, :], CT[:, g, cs], EA[i3][:], ALU.mult, [("CT", g), ("EA", i3)], [("Cs", h)])
        zi = q % 2
        zk = ("xz", zi)
        xv = xtok[:, q, :].rearrange("p (a b c) -> p a b c", b=2, c=64)
        xzv = xz[zi][:].rearrange("p (a b) c -> p a b c", b=2)
        self.copy(xzv[:, :, 0, 0:64], xv[:, :, 0, :], [("xtok", q)], [zk], eng="act")
        self.copy(xzv[:, :, 1, 64:128], xv[:, :, 1, :], [("xtok", q)], [zk], eng="act")
        hk_ = ("hsl", zi)
        self.ld(hsl[zi][:].rearrange("p a b -> p (a b)"), self.ssd_HS[q], [("ssd_HS", q, 0), ("ssd_HS", q, 1)], [hk_])
        for d in range(2):
            hv = hsl[zi][:, d, :].rearrange("p (a b c) -> p a b c", b=2, c=64)
            hzv = hz[zi][:, d].rearrange("p (a b) c -> p a b c", b=2)
            self.copy(hzv[:, :, 0, 0:64], hv[:, :, 0, :], [hk_], [zk], eng="dve")
            self.copy(hzv[:, :, 1, 64:128], hv[:, :, 1, :], [hk_], [zk], eng="dve")
        for pr in range(6):
            pp, kk_ = (psy0, pky0) if pr < 4 else (psy1, pky1)
            oreg = pp[:, (pr % 4) * 128:(pr % 4) * 128 + 128]
            if "ssd_B1" in self.debug:
                continue
            for hh in range(2):
                h = pr * 2 + hh
                rk = [zk, ("M", h), ("Cs", h)]
                self.mm(oreg, xz[zi][:, h, :], M[:, h, 0, :], hh == 0, False, rk, [kk_])
                self.mm(oreg, xz[zi][:, h, :], M[:, h, 1, :], False, False, rk, [kk_])
                self.mm(oreg, hz[zi][:, 0, h, :], Cs[:, h, 0, :], False, False, rk, [kk_])
                self.mm(oreg, hz[zi][:, 1, h, :], Cs[:, h, 1, :], False, hh == 1, rk, [kk_])
        for pr in range(6):
            pp, kk_ = (psy0, pky0) if pr < 4 else (psy1, pky1)
            self.stt(ysb[:, pr, tcol:tcol + 128], xs_t[:, pr, :], self.V("ss_d", pr),
                     pp[:, (pr % 4) * 128:(pr % 4) * 128 + 128], ALU.mult, ALU.add, [xsk, "vec", kk_], ["ysb"])
        if q % CPT == CPT - 1:
            t = q // CPT
            ts_ = slice(t * TT, (t + 1) * TT)
            self.ld(zt[:], zv[:, :, ts_], [("pT", rb) for rb in zrb], ["zt"])
            self.act(zt[:], zt[:], AF.Silu, ["zt"], ["zt"])
            self.tt(ysb[:], ysb[:], zt[:], ALU.mult, ["ysb", "zt"], ["ysb"])
            self.act(sqb[:], ysb[:], AF.Square, ["ysb"], ["sqb"])
            ps, pk = self.bank()
            for pr in range(6):
                self.mm(ps[:, :TT], self.ones768[:], sqb[:, pr, :], pr == 0, pr == 5, ["ones768", "sqb"], [pk])
            self.act(sd[:], ps[:, :TT], AF.Sqrt, [pk], ["sd2"], bias=EPS)
            self.recip(sd[:], sd[:], ["sd2"], ["sd2"])
            for pr in range(6):
                self.stt(yo[:, pr, :], ysb[:, pr, :], self.V("ss_ng", pr), sd[:], ALU.mult, ALU.mult,
                         ["ysb", "vec", "sd2"], ["yo"])
            self.st(self.ymix[1280:2048, ts_].rearrange("(a p) t -> p a t", p=128), yo[:], ["yo"],
                    [("ymix", i) for i in range(10, 16)])
    self.ps_pool = None
    self.pop()
    self.end()


RWC = 0.6065306597126334


class _Stop(Exception):
    pass


@_add_methods(B)
def rwkv_phase(self, l):
    try:
        self.rwkv_phase_(l)
    except _Stop:
        self.k.barrier()
        self.phase_es.close()
        for e_ in reversed(getattr(self, "stack", [])):
            e_.close()
        self.stack = []
        self.phase_es = None
        self.ps_pool = None


@_add_methods(B)
def rw_stop(self, n):
    for f_ in self.debug:
        if f_.startswith("rw_stop=") and int(f_[8:]) == n:
            raise _Stop()


@_add_methods(B)
def rwkv_phase_(self, l):
    L = self.L
    TT = min(256, L)
    NT = L // TT
    NC = L // 64
    C8 = TT // 64
    U = 8
    if not hasattr(self, "rwP"):
        self.rwP = self.dscr("rwP", [6, 64, NC * 256], BF16)
        self.rwQ = self.dscr("rwQ", [6, 64, NC * 256], BF16)
        self.rwR = self.dscr("rwR", [6, 64, NC * 256], BF16)
        self.rwY = self.dscr("rwY", [6, 64, NC * 128])
        self.rwG = self.dscr("rwG", [768, L])
        self.rwBG = self.dscr("rwBG", [768, L])
        self.rwYH = self.dscr("rwYH", [2, 64, 2 * min(256, L)])
        self.rw_st_in = self.dscr("rw_st_in", [64, 3072])
        self.rw_st_all = self.dscr("rw_st_all", [NCORE * 64, 3072])
    self.begin()
    self.ps_pool = [4, 5, 6, 7]
    up = self.sb("up", [128, 4, 768], BF16)
    for j in range(4):
        self.ldc(up[:, j, :], self.rw_up[l][:, j, :], [], ["up"])
    gup = self.sb("gup", [128, 2, 768], BF16)
    for j in range(2):
        self.ldc(gup[:, j, :], self.rw_gup[l][:, j, :], [], ["gup"])
    c0 = self.sb("c0", [128, 24], F32)
    self.ts(c0[:], self.V("rw_mu", 0, 24), -1.0, ALU.mult, ["vec"], ["c0"], s2=1.0, op1=ALU.add)
    self.tt(c0[:], c0[:], self.V("rw_mu", 24, 24), ALU.subtract, ["c0", "vec"], ["c0"])
    omka = self.sb("omka", [128, 6], F32)
    self.ts(omka[:], self.V("rw_ka", 0, 6), -1.0, ALU.mult, ["vec"], ["omka"], s2=1.0, op1=ALU.add)
    rst = self.sb("rst", [128, 2, TT], F32)
    self.ld(rst[:], self.rst_d[:, :, 0:TT], [], ["rst"])
    rwm = self.sb("rwm", [64, 2, 128], F32)
    self.ld(rwm[:], self.rwm_d, [], ["rwm"])
    rwn = self.sb("rwn", [64, 2, 64], F32)
    self.ld(rwn[:], self.rwn_d, [], ["rwn"])
    MK = self.sb("MK", [64, U, 128], F32)
    MKn = self.sb("MKn", [64, U, 64], F32)
    IDu = self.sb("IDu", [64, U, 64], BF16)
    for u in range(U):
        d = u % 2
        self.copy(MK[:, u, :], rwm[:, d, :], ["rwm"], ["MK"], eng="dve")
        self.copy(MKn[:, u, :], rwn[:, d, :], ["rwn"], ["MKn"], eng="dve")
        self.copy(IDu[:, u, :], self.C32("ident")[0:64, 0:64], ["cm32"], ["IDu"], eng="dve")
    ID2 = self.sb("ID2", [128, 64], F32)
    self.copy(ID2[0:64, :], self.C32("ident")[0:64, 0:64], ["cm32"], ["ID2"], eng="dve")
    self.copy(ID2[64:128, :], self.C32("ident")[64:128, 64:128], ["cm32"], ["ID2"], eng="dve")
    tw = [self.sb("tw", [128, L], BF16) for _ in range(2)]
    al = [self.sb("al", [128, L], BF16) for _ in range(2)]
    sg = self.sb("sg", [128, 2, L], BF16)
    PAY = self.sb("PAY", [64, 6, 2, 4, 64], F32)
    for d_ in range(2):
        self.k.op("pool", lambda e, d_=d_: e.memset(tw[d_][:], 0.0), [], [("tw", d_)])
        self.k.op("pool", lambda e, d_=d_: e.memset(al[d_][:], 0.0), [], [("al", d_)])

    def shift(out, xe, ti, nr, c_lo, n):
        k_ = ("xe", id(xe))
        self.ts(out, xe[0:nr, 1 + c_lo:1 + c_lo + n], c0[0:nr, ti:ti + 1], ALU.mult, [k_, "c0"], ["shf"])
        self.stt(out, xe[0:nr, c_lo:c_lo + n], self.V("rw_mu", ti, 1, 0, nr), out, ALU.mult, ALU.add,
                 [k_, "vec", "shf"], ["shf"])
        self.stt(out, xe[0:nr, 2 + c_lo:2 + c_lo + n], self.V("rw_mu", 24 + ti, 1, 0, nr), out, ALU.mult, ALU.add,
                 [k_, "vec", "shf"], ["shf"])

    self.push()
    xe = self.sb("xe", [128, L + 3], F32)
    tmpf = self.sb("tmpf", [128, L], F32)
    for ti in range(18, 24):
        r0, nr = RW_TILES[ti]
        self.load_ext(xe, RW0 + r0, nr, ("xe", id(xe)))
        shift(tmpf[0:nr, :], xe, ti, nr, 0, L)
        if ti < 20:
            self.act(tw[ti - 18][0:nr, :], tmpf[0:nr, :], AF.Tanh, ["shf"], [("tw", ti - 18)])
        elif ti < 22:
            self.act(al[ti - 20][0:nr, :], tmpf[0:nr, :], AF.Copy, ["shf"], [("al", ti - 20)])
        else:
            self.act(sg[:, ti - 22, :], tmpf[:], AF.Sigmoid, ["shf"], ["sg"])
    self.pop()

    self.rw_stop(1)

    def f32t(name):
        return self.sb(name, [128, TT], F32)

    for pr in range(6):
        self.push()
        xes = [self.sb("xe3", [128, L + 3], F32) for _ in range(3)]
        for j in range(3):
            self.load_ext(xes[j], RW0 + j * 768 + pr * 128, 128, ("xe", id(xes[j])))
        Pst = self.sb("Pst", [64, NC, 4, 64], BF16)
        Qst = self.sb("Qst", [64, NC, 4, 64], BF16)
        Rtt = self.sb("Rtt", [64, C8, 4, 64], BF16)
        Ytt = self.sb("Ytt", [64, C8, 2, 64], F32)
        r_, k_, v_ = f32t("r"), f32t("k"), f32t("v")
        kkn, sq, kk = f32t("kkn"), f32t("sq"), f32t("kk")
        s_, cs_, a_ = f32t("s"), f32t("cs"), f32t("a")
        kd, b_, ksum = f32t("kd"), f32t("b"), f32t("ksum")
        e_, t2, g_ = f32t("e"), f32t("t2"), f32t("g")
        Vb = self.sb("Vb", [128, TT], BF16)
        ZR = [self.sb("ZR", [128, C8, 2, 64], BF16) for _ in range(2)]
        BKt = [self.sb("BKt", [128, C8, 2, 64], BF16) for _ in range(2)]
        BKh = [self.sb("BKh", [128, C8, 2, 64], BF16) for _ in range(2)]
        DW = [self.sb("DW", [128, C8, 64], F32) for _ in range(2)]
        DWh = [self.sb("DWh", [128, C8, 64], BF16) for _ in range(2)]
        DWl = [self.sb("DWl", [128, C8, 64], BF16) for _ in range(2)]
        TK = self.sb("TK", [64, U, 4, 64], BF16)
        DGR = self.sb("DGR", [64, U, 2, 64], F32)
        ATb = self.sb("ATb", [64, U, 128], BF16)
        ATk = self.sb("ATk", [64, U, 128], BF16)
        NNp = [self.sb("NN", [64, U, 128], F32) for _ in range(2)]
        BBp = [self.sb("BB", [64, U, 128], F32) for _ in range(2)]
        MM = [self.sb("MM", [64, U, 64], F32) for _ in range(2)]
        B0p = self.sb("B0f", [64, U, 128], F32)
        for T_ in NNp + BBp + [B0p]:
            self.k.op("pool", lambda e, T_=T_: e.memset(T_[:], 0.0), [], [])
        self.k.barrier()

        class _V:
            def __init__(s_, t_):
                s_.t = t_

            def __getitem__(s_, idx):
                return s_.t[:, :, 0:64][idx] if False else s_.t[idx]
        NN = NNp
        BB = BBp
        B0f = B0p
        Mfb = self.sb("Mfb", [64, U, 64], BF16)
        X = self.sb("X", [64, U, 128], BF16)
        ZU = self.sb("ZU", [64, U, 128], BF16)
        Ytmp = self.sb("Ytmp", [64, 2, 2, 64], F32)
        c3 = lambda ap: ap.rearrange("p (c t) -> p c t", t=64)
        for t in range(NT):
            lo = t * TT
            shift(r_[:], xes[0], pr, 128, lo, TT)
            self.copy(r_[:], r_[:], ["shf"], ["r"], eng="act")
            shift(k_[:], xes[1], 6 + pr, 128, lo, TT)
            self.copy(k_[:], k_[:], ["shf"], ["k"], eng="act")
            shift(v_[:], xes[2], 12 + pr, 128, lo, TT)
            self.copy(Vb[:], v_[:], ["shf"], ["Vb"], eng="act")
            self.copy(v_[:], v_[:], ["shf"], ["v"], eng="act")
            self.ts(kkn[:], k_[:], self.V("rw_kk", pr), ALU.mult, ["k", "vec"], ["kkn"])
            self.tt(sq[:], kkn[:], kkn[:], ALU.mult, ["kkn"], ["sq"])
            ps, pk = self.bank()
            self.mm(ps[:, :TT], self.C32("bd64"), sq[:], True, True, ["cm32", "sq"], [pk])
            self.act(sq[:], ps[:, :TT], AF.Sqrt, [pk], ["sq"])
            self.ts(sq[:], sq[:], 1e-12, ALU.max, ["sq"], ["sq"])
            self.recip(sq[:], sq[:], ["sq"], ["sq"])
            self.tt(kk[:], kkn[:], sq[:], ALU.mult, ["kkn", "sq"], ["kk"])
            for d in range(2):
                col = 63 if d == 0 else 0
                ps, pk = self.bank()
                self.mm(ps[:, :TT], up[:, d, pr * 128:(pr + 1) * 128], tw[d][:, lo:lo + TT], True, True,
                        ["up", ("tw", d)], [pk])
                self.act(s_[:], ps[:, :TT], AF.Sigmoid, [pk, "vec"], ["s"], bias=self.V("rw_w0", d * 6 + pr))
                if d == 0:
                    self.scan(cs_[:], rst[:, 0, :], s_[:], 0.0, ["rst", "s"], ["cs"])
                else:
                    self.scan(cs_[:, ::-1], rst[:, 1, ::-1], s_[:, ::-1], 0.0, ["rst", "s"], ["cs"])
                ps, pk = self.bank()
                self.mm(ps[:, :TT], up[:, 2 + d, pr * 128:(pr + 1) * 128], al[d][:, lo:lo + TT], True, True,
                        ["up", ("al", d)], [pk])
                self.act(a_[:], ps[:, :TT], AF.Sigmoid, [pk, "vec"], ["a"], bias=self.V("rw_a0", d * 6 + pr))
                self.ts(t2[:], a_[:], self.V("rw_ka", pr), ALU.mult, ["a", "vec"], ["t2"], s2=omka[:, pr:pr + 1],
                        op1=ALU.add)
                self.tt(kd[:], k_[:], t2[:], ALU.mult, ["k", "t2"], ["kd"])
                self.tt(b_[:], kk[:], a_[:], ALU.mult, ["kk", "a"], ["b"])
                if d == 0:
                    self.copy(ksum[:], kd[:], ["kd"], ["ksum"], eng="act")
                else:
                    self.tt(ksum[:], ksum[:], kd[:], ALU.add, ["ksum", "kd"], ["ksum"])
                zk, bk, hk, dk = ("ZR", d), ("BKt", d), ("BKh", d), ("DW", d)
                self.act(e_[:], cs_[:], AF.Exp, ["cs"], ["e"], scale=-RWC)
                self.tt(ZR[d][:, :, 1, :], c3(r_[:]), c3(e_[:]), ALU.mult, ["r", "e"], [zk])
                self.tt(DW[d][:], ID2[:].unsqueeze(1).to_broadcast([128, C8, 64]),
                        c3(e_[:])[:, :, col:col + 1].to_broadcast([128, C8, 64]), ALU.mult, ["ID2", "e"], [dk])
                self.copy(DWh[d][:], DW[d][:], [dk], [("DWh", d)], eng="dve")
                self.tt(DWl[d][:], DW[d][:], DWh[d][:], ALU.subtract, [dk, ("DWh", d)], [("DWl", d)])
                self.tt(t2[:], cs_[:], s_[:], ALU.subtract, ["cs", "s"], ["t2"])
                self.act(e_[:], t2[:], AF.Exp, ["t2"], ["e"], scale=-RWC)
                self.stt(ZR[d][:, :, 0, :], c3(kk[:]), -1.0, c3(e_[:]), ALU.mult, ALU.mult, ["kk", "e"], [zk])
                self.act(e_[:], cs_[:], AF.Exp, ["cs"], ["e"], scale=RWC)
                self.tt(BKt[d][:, :, 1, :], c3(kd[:]), c3(e_[:]), ALU.mult, ["kd", "e"], [bk])
                self.tt(BKt[d][:, :, 0, :], c3(b_[:]), c3(e_[:]), ALU.mult, ["b", "e"], [bk])
                self.tt(c3(t2[:]), c3(cs_[:])[:, :, col:col + 1].to_broadcast([128, C8, 64]), c3(cs_[:]),
                        ALU.subtract, ["cs"], ["t2"])
                self.act(e_[:], t2[:], AF.Exp, ["t2"], ["e"], scale=-RWC)
                self.tt(BKh[d][:, :, 1, :], c3(kd[:]), c3(e_[:]), ALU.mult, ["kd", "e"], [hk])
                self.tt(BKh[d][:, :, 0, :], c3(b_[:]), c3(e_[:]), ALU.mult, ["b", "e"], [hk])
            self.tt(t2[:], r_[:], ksum[:], ALU.mult, ["r", "ksum"], ["t2"])
            self.ts(t2[:], t2[:], self.V("rw_rk", pr), ALU.mult, ["t2", "vec"], ["t2"])
            ps, pk = self.bank()
            self.mm(ps[:, :TT], self.C32("bd64"), t2[:], True, True, ["cm32", "t2"], [pk])
            self.tt(t2[:], ps[:, :TT], v_[:], ALU.mult, [pk, "v"], ["t2"])
            ps, pk = self.bank()
            for c in range(2):
                self.mm(ps[:, :TT], gup[:, c, pr * 128:(pr + 1) * 128], sg[:, c, lo:lo + TT], c == 0, c == 1,
                        ["gup", "sg"], [pk])
            self.copy(g_[:], ps[:, :TT], [pk], ["g"], eng="act")
            self.tt(t2[:], t2[:], g_[:], ALU.mult, ["t2", "g"], ["t2"])
            self.st(self.rwG[pr * 128:(pr + 1) * 128, lo:lo + TT], g_[:], ["g"], [("rwG", pr)])
            self.st(self.rwBG[pr * 128:(pr + 1) * 128, lo:lo + TT], t2[:], ["t2"], [("rwBG", pr)])
            self.rw_stop(2)
            for g in range(C8 // 2):
                c0g = t * C8 + g * 2
                units = [(ci, d, hh) for hh in range(2) for ci in range(2) for d in range(2)]
                for half in range(2):
                    pT_ = self.psum[2 + half]
                    pkT = [("ps", 4 + 2 * half), ("ps", 5 + 2 * half)]
                    pD_, pkD = self.bank(0 if half == 0 else 2)
                    for u in range(half * 4, half * 4 + 4):
                        ci, d, hh = units[u]
                        assert hh == half
                        cl = g * 2 + ci
                        pb = hh * 64
                        idb = self.C16("ident")[pb:pb + 64, pb:pb + 64]
                        srcs = [Vb[pb:pb + 64, cl * 64:(cl + 1) * 64], ZR[d][pb:pb + 64, cl, 0, :],
                                BKh[d][pb:pb + 64, cl, 0, :], BKh[d][pb:pb + 64, cl, 1, :]]
                        for q4 in range(4):
                            o = ((u % 4) * 4 + q4) * 64
                            self.mm(pT_[0:64, o:o + 64], srcs[q4], idb, True, True,
                                    ["Vb", ("ZR", d), ("BKh", d), "cm16"], [pkT[o // 512]])
                        o = (u % 4) * 128
                        self.mm(pD_[0:64, o:o + 64], DWh[d][pb:pb + 64, cl, :], idb, True, False,
                                [("DWh", d), "cm16"], [pkD])
                        self.mm(pD_[0:64, o:o + 64], DWl[d][pb:pb + 64, cl, :], idb, False, True,
                                [("DWl", d), "cm16"], [pkD])
                        self.mm(pD_[0:64, o + 64:o + 128], idb, ZR[d][pb:pb + 64, cl, 1, :], True, True,
                                [("ZR", d), "cm16"], [pkD])
                    for bk in range(2):
                        u0 = half * 4 + bk * 2
                        self.copy(TK[:, u0:u0 + 2].rearrange("p a b c -> p (a b c)"),
                                  pT_[0:64, bk * 512:(bk + 1) * 512], [pkT[bk]], ["TK"],
                                  eng="act" if half == 0 else "dve")
                    self.copy(DGR[:, half * 4:half * 4 + 4].rearrange("p a b c -> p (a b c)"), pD_[0:64, 0:512],
                              [pkD], ["DGR"], eng="dve" if half == 0 else "act")
                self.rw_stop(3)
                pA, pB = self.psum[0], self.psum[1]
                kA, kB = [("ps", 0), ("ps", 1)], [("ps", 2), ("ps", 3)]
                pN0, kN0 = self.bank()
                pN1, kN1 = self.bank()
                for u, (ci, d, hh) in enumerate(units):
                    cl = g * 2 + ci
                    pb = hh * 64
                    zr = ZR[d][pb:pb + 64, cl].rearrange("p a b -> p (a b)")
                    rk = [("ZR", d), ("BKt", d)]
                    self.mm(pA[0:64, u * 128:(u + 1) * 128], BKt[d][pb:pb + 64, cl, 0, :], zr, True, True, rk,
                            [kA[u // 4]])
                    self.mm(pB[0:64, u * 128:(u + 1) * 128], BKt[d][pb:pb + 64, cl, 1, :], zr, True, True, rk,
                            [kB[u // 4]])
                    pN_, kN_ = (pN0, kN0) if hh == 0 else (pN1, kN1)
                    self.mm(pN_[0:64, (u % 4) * 64:(u % 4) * 64 + 64], ZR[d][pb:pb + 64, cl, 0, :],
                            BKt[d][pb:pb + 64, cl, 0, :], True, True, rk, [kN_])
                fl = lambda ap: ap.rearrange("p a b -> p (a b)")
                for bk in range(2):
                    us = slice(bk * 4, bk * 4 + 4)
                    self.tt(fl(ATb[:, us]), pA[0:64, bk * 512:(bk + 1) * 512], fl(MK[:, us]), ALU.mult,
                            [kA[bk], "MK"], ["ATb"])
                    self.tt(fl(ATk[:, us]), pB[0:64, bk * 512:(bk + 1) * 512], fl(MK[:, us]), ALU.mult,
                            [kB[bk], "MK"], ["ATk"])
                    self.tt(B0f[:, us, 0:64], pA[0:64, bk * 512:(bk + 1) * 512].rearrange("p (a b) -> p a b", b=128)[:, :, 0:64],
                            MK[:, us, 0:64], ALU.mult, [kA[bk], "MK"], ["B0f"])
                v64 = lambda ap: ap.rearrange("p (a b) -> p a b", b=64)
                self.tt(NN[0][:, 0:4, 0:64], v64(pN0[0:64, 0:256]), MKn[:, 0:4], ALU.mult, [kN0, "MKn"], [("NN", 0)])
                self.tt(NN[0][:, 4:8, 0:64], v64(pN1[0:64, 0:256]), MKn[:, 4:8], ALU.mult, [kN1, "MKn"], [("NN", 0)])
                self.rw_stop(4)
                self.tt(MM[0][:], B0f[:, :, 0:64], IDu[:], ALU.add, ["B0f", "IDu"], [("MM", 0)])
                Bc, Bk = (lambda u: B0f[:, u, :]), "B0f"
                Nc, Nk = (lambda u: NN[0][:, u, :]), ("NN", 0)
                mi = 0
                for i in range(1, 6):
                    ni = i % 2
                    pN, kN = self.bank()
                    for u in range(U):
                        self.mm(pN[:, u * 64:(u + 1) * 64], Bc(u), Nc(u)[:, 0:64], True, True, [Bk, Nk], [kN])
                    if i <= 4:
                        pBn, kBn = self.bank()
                        for u in range(U):
                            self.mm(pBn[:, u * 64:(u + 1) * 64], Nc(u), Bc(u)[:, 0:64], True, True, [Bk, Nk], [kBn])
                    self.copy(NN[ni][:, :, 0:64], v64(pN[0:64, :]), [kN], [("NN", ni)], eng="act")
                    if i <= 4:
                        self.copy(BB[ni][:, :, 0:64], v64(pBn[0:64, :]), [kBn], [("BB", ni)], eng="dve")
                    pM, kM = self.bank()
                    for u in range(U):
                        self.mm(pM[:, u * 64:(u + 1) * 64], NN[ni][:, u, :], MM[mi][:, u, :], True, True,
                                [("NN", ni), ("MM", mi)], [kM])
                    self.tt(fl(MM[1 - mi][:]), pM[0:64, :], fl(MM[mi][:]), ALU.add, [kM, ("MM", mi)],
                            [("MM", 1 - mi)])
                    mi = 1 - mi
                    Bc, Bk = (lambda u, ni=ni: BB[ni][:, u, :]), ("BB", ni)
                    Nc, Nk = (lambda u, ni=ni: NN[ni][:, u, :]), ("NN", ni)
                self.copy(Mfb[:], MM[mi][:], [("MM", mi)], ["Mfb"], eng="act")
                Mf, Mk_ = Mfb, "Mfb"
                self.rw_stop(5)
                pX, kX = self.bank()
                for u in range(U):
                    self.mm(pX[0:64, u * 64:(u + 1) * 64], ATk[:, u, 0:64], TK[:, u, 0, :], True, True,
                            ["ATk", "TK"], [kX])
                self.copy(X[:, :, 64:128], pX[0:64, :].rearrange("p (a b) -> p a b", b=64), [kX], ["X"], eng="act")
                self.copy(X[:, :, 0:64], TK[:, :, 1, :], ["TK"], ["X"], eng="dve")
                for u in range(U):
                    self.mm(pA[0:64, u * 128:(u + 1) * 128], Mf[:, u, :], X[:, u, :], True, True, [Mk_, "X"],
                            [kA[u // 4]])
                for bk in range(2):
                    self.copy(fl(ZU[:, bk * 4:bk * 4 + 4]), pA[0:64, bk * 512:(bk + 1) * 512], [kA[bk]], ["ZU"],
                              eng="act" if bk == 0 else "dve")
                self.rw_stop(6)
                pP, kP = self.bank()
                pQ, kQ = self.bank()
                pR, kR = self.bank()
                pY, kY = self.bank()
                for u, (ci, d, hh) in enumerate(units):
                    cl = g * 2 + ci
                    pb = hh * 64
                    o = slice(u * 64, (u + 1) * 64)
                    self.mm(pP[0:64, o], ZU[:, u, 0:64], TK[:, u, 2, :], True, True, ["ZU", "TK"], [kP])
                    self.mm(pQ[0:64, o], TK[:, u, 2, :], ZU[:, u, 64:128], True, False, ["ZU", "TK"], [kQ])
                    self.mm(pQ[0:64, o], TK[:, u, 3, :], TK[:, u, 0, :], False, True, ["TK"], [kQ])
                    self.mm(pR[0:64, o], ZU[:, u, 0:64], ATb[:, u, 64:128], True, True, ["ZU", "ATb"], [kR])
                    self.mm(pY[0:64, o], ZU[:, u, 64:128], ATb[:, u, 64:128], True, False, ["ZU", "ATb"], [kY])
                    self.mm(pY[0:64, o], TK[:, u, 0, :], ATk[:, u, 64:128], False, True, ["TK", "ATk"], [kY])
                clg = g * 2
                for hh in range(2):
                    sv = lambda T_: (T_[:, clg:clg + 2] if T_ is Rtt else T_[:, c0g:c0g + 2]).rearrange(
                        "p c (d h) k -> p c d h k", d=2)[:, :, :, hh, :]
                    pv = lambda P_: P_[0:64, hh * 256:(hh + 1) * 256].rearrange("p (c d k) -> p c d k", c=2, d=2)
                    dg = DGR[:, hh * 4:hh * 4 + 4].rearrange("p (c d) q k -> p c d q k", c=2)
                    self.tt(sv(Pst), pv(pP), dg[:, :, :, 0, :], ALU.add, [kP, "DGR"], ["Pst"])
                    self.copy(sv(Qst), pv(pQ), [kQ], ["Qst"], eng="act")
                    self.tt(sv(Rtt), pv(pR), dg[:, :, :, 1, :], ALU.add, [kR, "DGR"], ["Rtt"])
                pYv = pY[0:64, :].rearrange("p (h c d t) -> p h c d t", h=2, c=2, d=2)
                self.copy(Ytmp[:], pYv[:, :, :, 1, :], [kY], ["Ytmp"], eng="dve")
                self.tt(Ytt[:, clg:clg + 2].rearrange("p c h t -> p h c t"), pYv[:, :, :, 0, :], Ytmp[:], ALU.add,
                        [kY, "Ytmp"], ["Ytt"])
            self.st(self.rwR[pr][:, t * C8 * 256:(t + 1) * C8 * 256], Rtt[:].rearrange("p a b c -> p (a b c)"),
                    ["Rtt"], [("rwR", pr)])
            self.st(self.rwY[pr][:, t * C8 * 128:(t + 1) * C8 * 128], Ytt[:].rearrange("p a b c -> p (a b c)"),
                    ["Ytt"], [("rwY", pr)])
        self.rw_stop(7)
        fl4 = lambda T_: T_[:].rearrange("p a b c -> p (a b c)")
        self.st(self.rwP[pr], fl4(Pst), ["Pst"], [("rwP", pr)])
        self.st(self.rwQ[pr], fl4(Qst), ["Qst"], [("rwQ", pr)])
        Sc = [self.sb("Sc", [64, 4, 64], BF16) for _ in range(2)]
        Gc = [self.sb("Gc", [64, 4, 64], BF16) for _ in range(2)]
        self.k.op("dve", lambda e: e.memset(Sc[0][:], 0.0), [], [("Sc", 0)])
        self.copy(Gc[0][:], IDu[:, 0:4, :], ["IDu"], [("Gc", 0)], eng="dve")
        for n in range(NC):
            ch = [n, NC - 1 - n]
            a, b2 = n % 2, 1 - n % 2
            pS, kS = self.bank()
            pG, kG = self.bank()
            for d in range(2):
                for hh in range(2):
                    j = d * 2 + hh
                    self.mm(pS[0:64, j * 64:(j + 1) * 64], Pst[:, ch[d], j, :], Sc[a][:, j, :], True, True,
                            ["Pst", ("Sc", a)], [kS])
                    self.mm(pG[0:64, j * 64:(j + 1) * 64], Pst[:, ch[d], j, :], Gc[a][:, j, :], True, True,
                            ["Pst", ("Gc", a)], [kG])
            for d in range(2):
                self.tt(Sc[b2][:, d * 2:d * 2 + 2, :].rearrange("p a b -> p (a b)"), pS[0:64, d * 128:(d + 1) * 128],
                        Qst[:, ch[d], d * 2:d * 2 + 2, :].rearrange("p a b -> p (a b)"), ALU.add,
                        [kS, "Qst"], [("Sc", b2)])
            self.copy(Gc[b2][:].rearrange("p a b -> p (a b)"), pG[0:64, 0:256], [kG], [("Gc", b2)], eng="act")
        fin = NC % 2
        self.copy(PAY[:, pr, 0], Sc[fin][:], [("Sc", fin)], ["PAY"], eng="dve")
        pGt, kGt = self.bank()
        for j in range(4):
            self.mm(pGt[0:64, j * 64:(j + 1) * 64], Gc[fin][:, j, :], self.C16("ident")[0:64, 0:64], True, True,
                    [("Gc", fin), "cm16"], [kGt])
        self.copy(PAY[:, pr, 1].rearrange("p a b -> p (a b)"), pGt[0:64, 0:256], [kGt], ["PAY"], eng="act")
        self.pop()
    self.rw_stop(8)
    self.st(self.rw_st_in, PAY[:].rearrange("p a b c d -> p (a b c d)"), ["PAY"], ["rst_in"])
    self.allgather(self.rw_st_in, self.rw_st_all, ["rst_in"], ["rst_all"])
    self.push()
    HIN = self.sb("HINr", [64, 2, 12, 64], F32)
    HINb = self.sb("HINrb", [64, 2, 12, 64], BF16)
    self.k.op("dve", lambda e: e.memset(HIN[:], 0.0), [], ["HINr"])
    PJ = [self.sb("PJ", [64, 6, 2, 4, 64], F32) for _ in range(2)]
    GJ = self.sb("GJ", [64, 6, 4, 64], BF16)
    S32 = self.sb("S32", [64, 12, 64], F32)
    S16 = self.sb("S16", [64, 12, 64], BF16)
    nld = 0
    for d in range(2):
        self.k.op("dve", lambda e: e.memset(S32[:], 0.0), [], ["S32"])
        self.k.op("dve", lambda e: e.memset(S16[:], 0.0), [], ["S16"])
        order = list(range(NCORE)) if d == 0 else list(range(NCORE - 1, -1, -1))
        for j in order:
            pj = PJ[nld % 2]
            pjk = ("PJ", nld % 2)
            nld += 1
            self.ld(pj[:].rearrange("p a b c d -> p (a b c d)"), self.rw_st_all[j * 64:(j + 1) * 64, :],
                    ["rst_all"], [pjk])
            self.copy(GJ[:], pj[:, :, 1], [pjk], ["GJ"], eng="act")
            self.stt(HIN[:, d], S32[:], self.flags[0:64, 16 + j:17 + j], HIN[:, d], ALU.mult, ALU.add,
                     ["S32", "flags", "HINr"], ["HINr"])
            pS0, kS0 = self.bank()
            pS1, kS1 = self.bank()
            for h in range(12):
                pp, kk_ = (pS0, kS0) if h < 8 else (pS1, kS1)
                self.mm(pp[0:64, (h % 8) * 64:(h % 8) * 64 + 64], GJ[:, h // 2, d * 2 + h % 2, :], S16[:, h, :],
                        True, True, ["GJ", "S16"], [kk_])
            Fv = pj[:, :, 0, d * 2:d * 2 + 2, :]
            S4 = S32[:].rearrange("p (a b) c -> p a b c", b=2)
            self.tt(S4[:, 0:4], pS0[0:64, :].rearrange("p (a b c) -> p a b c", b=2, c=64), Fv[:, 0:4], ALU.add,
                    [kS0, pjk], ["S32"])
            self.tt(S4[:, 4:6], pS1[0:64, 0:256].rearrange("p (a b c) -> p a b c", b=2, c=64), Fv[:, 4:6], ALU.add,
                    [kS1, pjk], ["S32"])
            lk = self.flags[0:64, d * 8 + j:d * 8 + j + 1]
            self.ts(S32[:], S32[:], lk, ALU.mult, ["S32", "flags"], ["S32"])
            self.copy(S16[:], S32[:], ["S32"], ["S16"], eng="act")
    self.copy(HINb[:], HIN[:], ["HINr"], ["HINrb"], eng="act")
    HINp = PAY[:].rearrange("p a b c d -> p (a b c d)").bitcast(BF16)
    self.copy(HINp[:, 0:1536], HINb[:].rearrange("p a b c -> p (a b c)"), ["HINrb"], ["PAY"], eng="dve")
    self.pop()
    HINv = HINp[:, 0:1536].rearrange("p (d h c) -> p d h c", d=2, h=12)
    self.rw_stop(9)
    for pr in range(6):
        self.push()
        Pst = self.sb("Pst2", [64, NC, 4, 64], BF16)
        Rst = self.sb("Rst2", [64, NC, 4, 64], BF16)
        Qst = self.sb("Qst2", [64, NC, 4, 64], BF16)
        Yst = self.sb("Yst2", [64, NC, 2, 64], F32)
        fl4 = lambda T_: T_[:].rearrange("p a b c -> p (a b c)")
        self.ld(fl4(Pst), self.rwP[pr], [("rwP", pr)], ["Pst"])
        self.ld(fl4(Qst), self.rwQ[pr], [("rwQ", pr)], ["Qst"])
        self.ld(fl4(Rst), self.rwR[pr], [("rwR", pr)], ["Rst"])
        self.ld(fl4(Yst), self.rwY[pr], [("rwY", pr)], ["Yst"])
        Sst = self.sb("Sst", [64, NC, 4, 64], BF16)
        for d in range(2):
            c_in = 0 if d == 0 else NC - 1
            self.copy(Sst[:, c_in, d * 2:d * 2 + 2, :], HINv[:, d, pr * 2:pr * 2 + 2, :], ["PAY"], [("Sst", c_in)],
                      eng="dve")
        for n in range(NC - 1):
            ch = [n, NC - 1 - n]
            nx = [n + 1, NC - 2 - n]
            pS, kS = self.bank()
            for d in range(2):
                for hh in range(2):
                    j = d * 2 + hh
                    self.mm(pS[0:64, j * 64:(j + 1) * 64], Pst[:, ch[d], j, :], Sst[:, ch[d], j, :], True, True,
                            ["Pst", ("Sst", ch[d])], [kS])
            for d in range(2):
                self.tt(Sst[:, nx[d], d * 2:d * 2 + 2, :].rearrange("p a b -> p (a b)"),
                        pS[0:64, d * 128:(d + 1) * 128],
                        Qst[:, ch[d], d * 2:d * 2 + 2, :].rearrange("p a b -> p (a b)"), ALU.add,
                        [kS, "Qst"], [("Sst", nx[d])])
        yh = self.sb("yh", [64, C8, 2, 64], F32)
        yp = self.sb("yp", [128, TT], F32)
        mean = self.sb("mean", [128, TT], F32)
        sq2 = self.sb("sq2", [128, TT], F32)
        gt = self.sb("gt", [128, TT], F32)
        bgt = self.sb("bgt", [128, TT], F32)
        yo = self.sb("yo2", [128, TT], BF16)
        for t in range(NT):
            lo = t * TT
            self.ld(gt[:], self.rwG[pr * 128:(pr + 1) * 128, lo:lo + TT], [("rwG", pr)], ["gt"])
            self.ld(bgt[:], self.rwBG[pr * 128:(pr + 1) * 128, lo:lo + TT], [("rwBG", pr)], ["bgt"])
            pY2 = self.psum[0]
            kY2 = [("ps", 0), ("ps", 1)]
            for cl in range(C8):
                c = t * C8 + cl
                for hh in range(2):
                    o = (cl * 2 + hh) * 64
                    self.mm(pY2[0:64, o:o + 64], Sst[:, c, hh, :], Rst[:, c, hh, :], True, False,
                            [("Sst", c), "Rst"], [kY2[o // 512]])
                    self.mm(pY2[0:64, o:o + 64], Sst[:, c, 2 + hh, :], Rst[:, c, 2 + hh, :], False, True,
                            [("Sst", c), "Rst"], [kY2[o // 512]])
            self.tt(yh[:].rearrange("p a b c -> p (a b c)"), pY2[0:64, 0:C8 * 128],
                    Yst[:, t * C8:(t + 1) * C8].rearrange("p a b c -> p (a b c)"), ALU.add, kY2 + ["Yst"], ["yh"])
            bi = t % 2
            self.st(self.rwYH[bi], yh[:].rearrange("p a b c -> p (a b c)"), ["yh"], [("rwYH", bi)])
            yhv = self.rwYH[bi].rearrange("p (c h t) -> p c h t", h=2, t=64)
            for hh in range(2):
                self.ld(yp[hh * 64:(hh + 1) * 64, :].rearrange("p (c t) -> p c t", t=64), yhv[:, :, hh, :],
                        [("rwYH", bi)], ["yp"])
            ps, pk = self.bank()
            self.mm(ps[:, :TT], self.bd64s[:], yp[:], True, True, ["bd64s", "yp"], [pk])
            self.tt(yp[:], yp[:], ps[:, :TT], ALU.subtract, ["yp", pk], ["yp"])
            self.tt(sq2[:], yp[:], yp[:], ALU.mult, ["yp"], ["sq2"])
            ps, pk = self.bank()
            self.mm(ps[:, :TT], self.bd64s[:], sq2[:], True, True, ["bd64s", "sq2"], [pk])
            self.act(sq2[:], ps[:, :TT], AF.Sqrt, [pk], ["sq2"], bias=64e-5)
            self.recip(sq2[:], sq2[:], ["sq2"], ["sq2"])
            self.tt(yp[:], yp[:], sq2[:], ALU.mult, ["yp", "sq2"], ["yp"])
            self.ts(yp[:], yp[:], self.V("rw_lng", pr), ALU.mult, ["yp", "vec"], ["yp"], s2=self.V("rw_lnb", pr),
                    op1=ALU.add)
            self.tt(yp[:], yp[:], gt[:], ALU.mult, ["yp", "gt"], ["yp"])
            self.tt(yo[:], yp[:], bgt[:], ALU.add, ["yp", "bgt"], ["yo2"])
            self.st(self.ymix[512 + pr * 128:512 + (pr + 1) * 128, lo:lo + TT], yo[:], ["yo2"], [("ymix", 4 + pr)])
        self.pop()
    self.ps_pool = None
    self.end()
```
